# Optimizing a Trainium2 kernel written in Bass

```python
import jax, jax.numpy as jnp
from jax import lax
import numpy as np

D_MODEL = 2048
BATCH = 16
SEQ = 256
DEPTH = 4
DEC_BATCH = 4
DEC_SEQ = 1024
PAST_LEN = 256

GRID_W = 64
HEAD_DIM = 128
A_HEADS = 8
A_KV_HEADS = 2
A_WINDOW = 128
A_BLOCK = 128
B_HEADS = 8
NA_ROWS_MAX = 8
NA_COLS = 16
NA_QCOLS = 16
NA_KCOLS = 32
C_HEADS = 8
C_DK = D_MODEL // C_HEADS
C_DV = D_MODEL // C_HEADS
C_CHUNK = 128
Q_BLOCK = 128
ROPE_THETA = 10000.0
EPS = 1e-6

N_EVEN = (DEPTH + 1) // 2
N_ODD = DEPTH // 2
A_Q = A_HEADS * HEAD_DIM
A_KV = A_KV_HEADS * HEAD_DIM
B_W = B_HEADS * HEAD_DIM
EVEN_WIDTH = A_Q + B_W
EVEN_IN = A_Q + 2 * A_KV + 3 * B_W + EVEN_WIDTH
C_WIDTH = C_HEADS * C_DV
ODD_IN = 2 * C_HEADS * C_DK + 2 * C_WIDTH
RPB_R = 2 * NA_ROWS_MAX - 1
RPB_C = 2 * NA_COLS - 1

kernel_name = "hybrid_diffusion_prefix_step"


def rmsnorm(x, g):
    xf = x.astype(jnp.float32)
    y = xf * lax.rsqrt(jnp.mean(xf * xf, axis=-1, keepdims=True) + EPS)
    return (y * g.astype(jnp.float32)).astype(x.dtype)


def adaln(cvec, w, b):
    mod = jax.nn.silu(cvec) @ w + b
    return jnp.split(mod, 3, axis=-1)


def _rot(x, ang):
    n = x.shape[-1] // 2
    cos = jnp.cos(ang)[None, :, None, :]
    sin = jnp.sin(ang)[None, :, None, :]
    x1, x2 = x[..., :n], x[..., n:]
    return jnp.concatenate([x1 * cos - x2 * sin, x2 * cos + x1 * sin], axis=-1)


def axial_rope(x):
    T = x.shape[1]
    t = jnp.arange(T)
    half = HEAD_DIM // 2
    nf = half // 2
    inv = ROPE_THETA ** (-jnp.arange(nf, dtype=jnp.float32) / nf)
    ang_r = (t // GRID_W).astype(jnp.float32)[:, None] * inv[None]
    ang_c = (t % GRID_W).astype(jnp.float32)[:, None] * inv[None]
    xf = x.astype(jnp.float32)
    return jnp.concatenate([_rot(xf[..., :half], ang_r), _rot(xf[..., half:], ang_c)], axis=-1).astype(x.dtype)


def context_attention(q, k, v, sink):
    B, T, H, hd = q.shape
    Hkv, Tk = k.shape[1], k.shape[2]
    G = H // Hkv
    nqb = T // Q_BLOCK
    qb = q.reshape(B, nqb, Q_BLOCK, Hkv, G, hd).transpose(1, 0, 2, 3, 4, 5)
    scale = hd ** -0.5

    def block(qi):
        s = jnp.einsum('bqngd,bnkd->bngqk', qi, k).astype(jnp.float32) * scale
        if sink is not None:
            sk = jnp.broadcast_to(sink.reshape(Hkv, G).astype(jnp.float32)[None, :, :, None, None], s.shape[:-1] + (1,))
            s = jnp.concatenate([s, sk], axis=-1)
        p = jax.nn.softmax(s, axis=-1)[..., :Tk].astype(v.dtype)
        return jnp.einsum('bngqk,bnkd->bqngd', p, v).reshape(B, Q_BLOCK, H * hd)

    o = lax.map(block, qb)
    return o.transpose(1, 0, 2, 3).reshape(B, T, H * hd)


def window_attention(q, k, v, ck, cv, sink):
    B, T, H, hd = q.shape
    Hkv = k.shape[2]
    G = H // Hkv
    nb = T // A_BLOCK
    P = ck.shape[2]
    scale = hd ** -0.5

    def band(x):
        xp = jnp.pad(x, ((0, 0), (A_BLOCK, A_BLOCK), (0, 0), (0, 0))).reshape(B, nb + 2, A_BLOCK, Hkv, hd)
        return jnp.concatenate([xp[:, :-2], xp[:, 1:-1], xp[:, 2:]], axis=2)

    kw, vw = band(k), band(v)
    qb = q.reshape(B, nb, A_BLOCK, Hkv, G, hd)
    blk = jnp.arange(nb)[:, None]
    qpos = blk * A_BLOCK + jnp.arange(A_BLOCK)[None]
    kpos = (blk - 1) * A_BLOCK + jnp.arange(3 * A_BLOCK)[None]
    valid = ((jnp.abs(qpos[:, :, None] - kpos[:, None, :]) <= A_WINDOW)
             & (kpos[:, None, :] >= 0) & (kpos[:, None, :] < T))
    s_w = jnp.einsum('bjqngd,bjknd->bjngqk', qb, kw).astype(jnp.float32) * scale
    s_w = jnp.where(valid[None, :, None, None], s_w, -jnp.inf)
    s_c = jnp.einsum('bjqngd,bnkd->bjngqk', qb, ck).astype(jnp.float32) * scale
    s_sink = jnp.broadcast_to(sink.reshape(Hkv, G).astype(jnp.float32)[None, None, :, :, None, None], s_c.shape[:-1] + (1,))
    p = jax.nn.softmax(jnp.concatenate([s_c, s_w, s_sink], axis=-1), axis=-1)
    pc = p[..., :P].astype(v.dtype)
    pw = p[..., P:P + 3 * A_BLOCK].astype(v.dtype)
    o = jnp.einsum('bjngqk,bnkd->bjqngd', pc, cv) + jnp.einsum('bjngqk,bjknd->bjqngd', pw, vw)
    return o.reshape(B, T, H * hd)


def neighbourhood_attention(q, k, v, ck, cv, rpb):
    B, T, H, hd = q.shape
    rows = T // GRID_W
    wr = min(NA_ROWS_MAX, rows)
    ncb = GRID_W // NA_QCOLS
    scale = hd ** -0.5
    r = jnp.arange(rows)
    r0 = jnp.clip(r - wr // 2, 0, rows - wr)
    krow = r0[:, None] + jnp.arange(wr)[None]
    m = jnp.arange(ncb)
    cs = jnp.clip(m * NA_QCOLS - NA_COLS // 2, 0, GRID_W - NA_KCOLS)
    kcol = cs[:, None] + jnp.arange(NA_KCOLS)[None]
    qc = m[:, None] * NA_QCOLS + jnp.arange(NA_QCOLS)[None]
    c0 = jnp.clip(qc - NA_COLS // 2, 0, GRID_W - NA_COLS)
    colmask = (kcol[:, None, :] >= c0[:, :, None]) & (kcol[:, None, :] < c0[:, :, None] + NA_COLS)
    dr_idx = krow - r[:, None] + (NA_ROWS_MAX - 1)
    dc_idx = jnp.clip(kcol[:, None, :] - qc[:, :, None] + (NA_COLS - 1), 0, RPB_C - 1)
    bias = rpb[:, dr_idx[:, None, None, :, None], dc_idx[None, :, :, None, :]].astype(jnp.float32)
    bias = jnp.where(colmask[None, None, :, :, None, :], bias, -jnp.inf)
    bias = bias.reshape(H, rows, ncb, NA_QCOLS, wr * NA_KCOLS).transpose(1, 2, 0, 3, 4)

    def gather(x):
        xg = x.reshape(B, rows, GRID_W, H, hd)
        xn = xg[:, krow[:, None, :, None], kcol[None, :, None, :]]
        return xn.reshape(B, rows, ncb, wr * NA_KCOLS, H, hd)

    kn, vn = gather(k), gather(v)
    qg = q.reshape(B, rows, ncb, NA_QCOLS, H, hd)
    P = ck.shape[2]
    s_n = jnp.einsum('brmqhd,brmkhd->brmhqk', qg, kn).astype(jnp.float32) * scale + bias[None]
    s_c = jnp.einsum('brmqhd,bhkd->brmhqk', qg, ck).astype(jnp.float32) * scale
    p = jax.nn.softmax(jnp.concatenate([s_c, s_n], axis=-1), axis=-1)
    pc = p[..., :P].astype(v.dtype)
    pn = p[..., P:].astype(v.dtype)
    o = jnp.einsum('brmhqk,bhkd->brmqhd', pc, cv) + jnp.einsum('brmhqk,brmkhd->brmqhd', pn, vn)
    return o.reshape(B, T, H * hd)


def _even_split(h, w_in):
    B, T = h.shape[:2]
    proj = h @ w_in
    idx = np.cumsum([A_Q, A_KV, A_KV, B_W, B_W, B_W])
    qa, ka, va, qb, kb, vb, gate = jnp.split(proj, idx, axis=-1)
    qa = qa.reshape(B, T, A_HEADS, HEAD_DIM)
    ka = ka.reshape(B, T, A_KV_HEADS, HEAD_DIM)
    va = va.reshape(B, T, A_KV_HEADS, HEAD_DIM)
    qb = qb.reshape(B, T, B_HEADS, HEAD_DIM)
    kb = kb.reshape(B, T, B_HEADS, HEAD_DIM)
    vb = vb.reshape(B, T, B_HEADS, HEAD_DIM)
    return qa, ka, va, qb, kb, vb, gate


def even_context(h, w_in, w_out, sink):
    qa, ka, va, qb, kb, vb, gate = _even_split(h, w_in)
    ka, va, kb, vb = (x.transpose(0, 2, 1, 3) for x in (ka, va, kb, vb))
    oa = context_attention(qa, ka, va, sink)
    ob = context_attention(qb, kb, vb, None)
    out = (jnp.concatenate([oa, ob], axis=-1) * jax.nn.silu(gate)) @ w_out
    return out, ka, va, kb, vb


def even_latent(h, w_in, w_out, sink, rpb, ck_a, cv_a, ck_b, cv_b):
    qa, ka, va, qb, kb, vb, gate = _even_split(h, w_in)
    qa, ka = axial_rope(qa), axial_rope(ka)
    oa = window_attention(qa, ka, va, ck_a, cv_a, sink)
    ob = neighbourhood_attention(qb, kb, vb, ck_b, cv_b, rpb)
    return (jnp.concatenate([oa, ob], axis=-1) * jax.nn.silu(gate)) @ w_out


def retention_scan(q, k, v, log_g, s0):
    B, T, H, dk = q.shape
    dv = v.shape[-1]
    nc = T // C_CHUNK

    def chunks(x):
        return x.astype(jnp.float32).reshape(B, nc, C_CHUNK, H, x.shape[-1]).transpose(1, 0, 2, 3, 4)

    qc, kc, vc = chunks(q), chunks(k), chunks(v)
    i = jnp.arange(C_CHUNK, dtype=jnp.float32)
    diff = i[:, None] - i[None, :]
    dmat = jnp.where(diff >= 0, jnp.exp(log_g[:, None, None] * jnp.maximum(diff, 0.0)), 0.0)
    q_dec = jnp.exp(log_g[None, :] * (i[:, None] + 1.0))
    k_dec = jnp.exp(log_g[None, :] * (C_CHUNK - 1.0 - i[:, None]))
    chunk_dec = jnp.exp(log_g * C_CHUNK)

    def step(S, inp):
        qi, ki, vi = inp
        s = jnp.einsum('bihd,bjhd->bhij', qi, ki) * dmat[None]
        inner = jnp.einsum('bhij,bjhe->bihe', s, vi)
        cross = jnp.einsum('bihd,bhde->bihe', qi, S) * q_dec[None, :, :, None]
        S_new = S * chunk_dec[None, :, None, None] + jnp.einsum('bjhd,bjhe->bhde', ki * k_dec[None, :, :, None], vi)
        return S_new, inner + cross

    S_fin, outs = lax.scan(step, s0.astype(jnp.float32), (qc, kc, vc))
    return outs.transpose(1, 0, 2, 3, 4).reshape(B, T, H, dv), S_fin


def retention_mixer(h, w_in, w_out, dec_f, dec_b, gn_g, s_f0, s_b0):
    B, T = h.shape[:2]
    q, k, v, gate = jnp.split(h @ w_in, 4, axis=-1)
    q = q.reshape(B, T, C_HEADS, C_DK)
    k = k.reshape(B, T, C_HEADS, C_DK) * (C_DK ** -0.5)
    v = v.reshape(B, T, C_HEADS, C_DV)
    log_f = -jnp.exp(dec_f.astype(jnp.float32))
    log_b = -jnp.exp(dec_b.astype(jnp.float32))
    o_f, S_f = retention_scan(q, k, v, log_f, s_f0)
    o_bf, S_b = retention_scan(jnp.flip(q, 1), jnp.flip(k, 1), jnp.flip(v, 1), log_b, s_b0)
    o = o_f + jnp.flip(o_bf, 1)
    mu = jnp.mean(o, axis=-1, keepdims=True)
    var = jnp.mean((o - mu) ** 2, axis=-1, keepdims=True)
    o = ((o - mu) * lax.rsqrt(var + EPS)).reshape(B, T, C_WIDTH) * gn_g.astype(jnp.float32)
    out = (o.astype(h.dtype) * jax.nn.silu(gate)) @ w_out
    return out, S_f, S_b


def setup_inputs(seed: int = 0) -> dict:
    key = jax.random.key(seed)
    ks = jax.random.split(key, 24)
    f32 = jnp.float32
    nrm = lambda k, shape, s: jax.random.normal(k, shape, f32) * s
    base_dec = jnp.log(-jnp.log1p(-(2.0 ** (-5.0 - jnp.arange(C_HEADS, dtype=f32)))))
    return {
        "x_prompt": nrm(ks[0], (BATCH, SEQ, D_MODEL), 1.0),
        "x_sample": nrm(ks[1], (DEC_BATCH, DEC_SEQ, D_MODEL), 1.0),
        "c": nrm(ks[2], (DEC_BATCH, D_MODEL), 1.0),
        "cache_a_k": nrm(ks[3], (DEC_BATCH, N_EVEN, A_KV_HEADS, PAST_LEN, HEAD_DIM), 1.0),
        "cache_a_v": nrm(ks[4], (DEC_BATCH, N_EVEN, A_KV_HEADS, PAST_LEN, HEAD_DIM), 1.0),
        "cache_b_k": nrm(ks[5], (DEC_BATCH, N_EVEN, B_HEADS, PAST_LEN, HEAD_DIM), 1.0),
        "cache_b_v": nrm(ks[6], (DEC_BATCH, N_EVEN, B_HEADS, PAST_LEN, HEAD_DIM), 1.0),
        "state_ret_f": nrm(ks[7], (DEC_BATCH, N_ODD, C_HEADS, C_DK, C_DV), 0.5),
        "state_ret_b": nrm(ks[8], (DEC_BATCH, N_ODD, C_HEADS, C_DK, C_DV), 0.5),
        "c_ctx": nrm(ks[9], (D_MODEL,), 1.0),
        "w_ada": nrm(ks[10], (DEPTH, D_MODEL, 3 * D_MODEL), 0.5 * D_MODEL ** -0.5),
        "b_ada": nrm(ks[11], (DEPTH, 3 * D_MODEL), 0.01),
        "norm_pre": 1.0 + nrm(ks[12], (DEPTH, D_MODEL), 0.05),
        "norm_post": 1.0 + nrm(ks[13], (DEPTH, D_MODEL), 0.05),
        "w_in_even": nrm(ks[14], (N_EVEN, D_MODEL, EVEN_IN), D_MODEL ** -0.5),
        "w_out_even": nrm(ks[15], (N_EVEN, EVEN_WIDTH, D_MODEL), EVEN_WIDTH ** -0.5),
        "a_sink": nrm(ks[16], (N_EVEN, A_HEADS), 0.5),
        "na_rpb": nrm(ks[17], (N_EVEN, B_HEADS, RPB_R, RPB_C), 0.1),
        "w_in_odd": nrm(ks[18], (N_ODD, D_MODEL, ODD_IN), D_MODEL ** -0.5),
        "w_out_odd": nrm(ks[19], (N_ODD, C_WIDTH, D_MODEL), C_WIDTH ** -0.5),
        "ret_decay_f": base_dec[None] + nrm(ks[20], (N_ODD, C_HEADS), 0.1),
        "ret_decay_b": base_dec[None] + nrm(ks[21], (N_ODD, C_HEADS), 0.1),
        "ret_gn": 1.0 + nrm(ks[22], (N_ODD, C_WIDTH), 0.05),
    }


def reference(x_prompt, x_sample, c, cache_a_k, cache_a_v, cache_b_k, cache_b_v, state_ret_f, state_ret_b,
              c_ctx, w_ada, b_ada, norm_pre, norm_post, w_in_even, w_out_even, a_sink, na_rpb,
              w_in_odd, w_out_odd, ret_decay_f, ret_decay_b, ret_gn):
    xp, xs = x_prompt, x_sample
    zero_state = jnp.zeros((x_prompt.shape[0], C_HEADS, C_DK, C_DV), jnp.float32)
    new_ak, new_av, new_bk, new_bv, new_sf, new_sb = [], [], [], [], [], []
    for l in range(DEPTH):
        i = l // 2
        sh_p, sc_p, g_p = adaln(c_ctx[None, :], w_ada[l], b_ada[l])
        sh_s, sc_s, g_s = adaln(c, w_ada[l], b_ada[l])
        hp = rmsnorm(xp, norm_pre[l]) * (1.0 + sc_p[:, None]) + sh_p[:, None]
        hs = rmsnorm(xs, norm_pre[l]) * (1.0 + sc_s[:, None]) + sh_s[:, None]
        if l % 2 == 0:
            op, ka, va, kb, vb = even_context(hp, w_in_even[i], w_out_even[i], a_sink[i])
            os_ = even_latent(hs, w_in_even[i], w_out_even[i], a_sink[i], na_rpb[i],
                              cache_a_k[:, i], cache_a_v[:, i], cache_b_k[:, i], cache_b_v[:, i])
            new_ak.append(ka)
            new_av.append(va)
            new_bk.append(kb)
            new_bv.append(vb)
        else:
            op, sf, sb = retention_mixer(hp, w_in_odd[i], w_out_odd[i], ret_decay_f[i], ret_decay_b[i],
                                         ret_gn[i], zero_state, zero_state)
            os_, _, _ = retention_mixer(hs, w_in_odd[i], w_out_odd[i], ret_decay_f[i], ret_decay_b[i],
                                        ret_gn[i], state_ret_f[:, i], state_ret_b[:, i])
            new_sf.append(sf)
            new_sb.append(sb)
        xp = xp + g_p[:, None] * rmsnorm(op, norm_post[l])
        xs = xs + g_s[:, None] * rmsnorm(os_, norm_post[l])
    new_a_k = jnp.stack(new_ak, axis=1)
    new_a_v = jnp.stack(new_av, axis=1)
    new_b_k = jnp.stack(new_bk, axis=1)
    new_b_v = jnp.stack(new_bv, axis=1)
    new_ret_f = jnp.stack(new_sf, axis=1)
    new_ret_b = jnp.stack(new_sb, axis=1)
    return (xp, xs, new_a_k, new_a_v, new_b_k, new_b_v, new_ret_f, new_ret_b)
```

```python
import contextlib
import numpy as np
import concourse.bass as bass
import concourse.mybir as mybir
from concourse.bass_utils import run_bass_kernel_spmd

F32 = mybir.dt.float32
BF16 = mybir.dt.bfloat16
AF = mybir.ActivationFunctionType
ALU = mybir.AluOpType

ENGS = ("pe", "act", "dve", "pool", "sp")
NEG = -30000.0
EPS = 1e-6
SCALE = 128.0 ** -0.5
NLAYERS = 4
import os
KDBG = os.environ.get("KDBG", "").split(",")


class Ev:
    __slots__ = ("eng", "idx", "sem", "val")

    def __init__(self, eng=None, idx=None, sem=None, val=None):
        self.eng = eng
        self.idx = idx
        self.sem = sem
        self.val = val


class Tile:
    def __init__(self, ap, name="", off=0, rowlen=0, handle=None):
        self.ap = ap
        self.name = name
        self.w = None
        self.r = []
        self.off = off
        self.rowlen = rowlen
        self.handle = handle

    def inherit(self, others):
        best = {}
        for o in others:
            evs = list(o.r)
            if o.w is not None:
                evs.append(o.w)
            for ev in evs:
                if ev.eng is not None:
                    k = ("c", ev.eng)
                    if k not in best or best[k].idx < ev.idx:
                        best[k] = ev
                else:
                    k = ("d", ev.sem)
                    if k not in best or best[k].val < ev.val:
                        best[k] = ev
        self.r.extend(best.values())
        return self


class OpRec:
    __slots__ = ("fn", "deps", "dma_sem", "dma_val", "signaled")

    def __init__(self, fn, deps):
        self.fn = fn
        self.deps = deps
        self.dma_sem = None
        self.dma_val = None
        self.signaled = False


class Prog:
    def __init__(self, nc, n_dma_sems=12):
        self.nc = nc
        self.ops = {e: [] for e in ENGS}
        self.n_dma_sems = n_dma_sems
        self.dma_rr = {"sp": 0, "pool": 0}
        self.dma_cnt = {}
        self.out_events = []

    def _deps(self, reads, writes):
        deps = []
        for t in reads:
            if t.w is not None:
                deps.append(t.w)
        for t in writes:
            if t.w is not None:
                deps.append(t.w)
            deps.extend(t.r)
        return deps

    def op(self, eng, fn, reads=(), writes=()):
        rec = OpRec(fn, self._deps(reads, writes))
        idx = len(self.ops[eng])
        self.ops[eng].append(rec)
        ev = Ev(eng=eng, idx=idx)
        for t in reads:
            t.r.append(ev)
        for t in writes:
            t.w = ev
            t.r = []
        return ev

    def dma(self, q, out_ap, in_ap, reads=(), writes=(), is_output=False):
        if "poolall" in KDBG or ("poolout" in KDBG and is_output):
            q = "pool"
        slot = self.dma_rr[q]
        self.dma_rr[q] = (slot + 1) % self.n_dma_sems
        key = (q, slot)
        cnt = self.dma_cnt.get(key, 0) + 1
        self.dma_cnt[key] = cnt
        deps = self._deps(reads, writes)
        if cnt > 1:
            deps.append(Ev(sem=key, val=16 * (cnt - 1)))

        def fn(e, out_ap=out_ap, in_ap=in_ap):
            return e.dma_start(out=out_ap, in_=in_ap)

        rec = OpRec(fn, deps)
        rec.dma_sem = key
        rec.dma_val = 16 * cnt
        self.ops[q].append(rec)
        ev = Ev(sem=key, val=16 * cnt)
        for t in reads:
            t.r.append(ev)
        for t in writes:
            t.w = ev
            t.r = []
        if is_output:
            self.out_events.append(ev)
        return ev

    def finish(self):
        rec = OpRec(None, list(self.out_events))
        self.ops["sp"].append(rec)

    def emit(self):
        nc = self.nc
        for e in ENGS:
            for rec in self.ops[e]:
                for d in rec.deps:
                    if d.eng is not None:
                        self.ops[d.eng][d.idx].signaled = True
        counts = {}
        for e in ENGS:
            c = 0
            arr = []
            for rec in self.ops[e]:
                if rec.signaled:
                    c += 1
                arr.append(c)
            counts[e] = arr
        with contextlib.ExitStack() as st:
            psem = {e: st.enter_context(nc.semaphore("p_" + e)) for e in ENGS}
            dsem = {}
            for q in ("sp", "pool"):
                for s in range(self.n_dma_sems):
                    if (q, s) in self.dma_cnt:
                        dsem[(q, s)] = st.enter_context(nc.semaphore("d_%s_%d" % (q, s)))
            block = st.enter_context(nc.Block())

            def run(ename, eh):
                waited = {}
                for rec in self.ops[ename]:
                    for d in rec.deps:
                        if d.eng is not None:
                            if d.eng == ename and ename == "pe":
                                continue
                            k = ("c", d.eng)
                            v = counts[d.eng][d.idx]
                            sem = psem[d.eng]
                        else:
                            k = ("d", d.sem)
                            v = d.val
                            sem = dsem[d.sem]
                        if waited.get(k, 0) >= v:
                            continue
                        waited[k] = v
                        eh.wait_ge(sem, v)
                    if rec.fn is None:
                        continue
                    ins = rec.fn(eh)
                    if rec.dma_sem is not None:
                        ins.then_inc(dsem[rec.dma_sem], 16)
                    elif rec.signaled:
                        ins.then_inc(psem[ename], 1)

            @block.tensor
            def _(eh):
                run("pe", eh)

            @block.scalar
            def _(eh):
                run("act", eh)

            @block.vector
            def _(eh):
                run("dve", eh)

            @block.gpsimd
            def _(eh):
                run("pool", eh)

            @block.sync
            def _(eh):
                run("sp", eh)


def mmg(items):
    def fn(e):
        ins = None
        for (o, l, r, s, t) in items:
            ins = e.matmul(o, l, r, start=s, stop=t)
        return ins
    return fn


def trg(items):
    def fn(e):
        ins = None
        for (o, i, ident) in items:
            ins = e.transpose(o, i, ident)
        return ins
    return fn


def actf(out, in_, func, bias=None, scale=None):
    kw = {}
    if bias is not None:
        kw["bias"] = bias
    if scale is not None:
        kw["scale"] = scale
    return lambda e: e.activation(out=out, in_=in_, func=func, **kw)


def tt(out, in0, in1, op):
    return lambda e: e.tensor_tensor(out=out, in0=in0, in1=in1, op=op)


def ts(out, in0, s1, s2, op0, op1=None):
    if op1 is None:
        return lambda e: e.tensor_scalar(out=out, in0=in0, scalar1=s1, scalar2=None, op0=op0)
    return lambda e: e.tensor_scalar(out=out, in0=in0, scalar1=s1, scalar2=s2, op0=op0, op1=op1)


def stt(out, in0, scalar, in1, op0, op1):
    return lambda e: e.scalar_tensor_tensor(out=out, in0=in0, scalar=scalar, in1=in1, op0=op0, op1=op1)


def cpy(out, in_):
    return lambda e: e.tensor_copy(out, in_)


def recip(out, in_):
    return lambda e: e.reciprocal(out=out, in_=in_)


class AliasTile(Tile):
    def __init__(self, base, ap):
        self.base = base
        self.ap = ap
        self.name = base.name
        self.off = base.off
        self.rowlen = base.rowlen
        self.handle = base.handle

    @property
    def w(self):
        return self.base.w

    @w.setter
    def w(self, v):
        self.base.w = v

    @property
    def r(self):
        return self.base.r

    @r.setter
    def r(self, v):
        self.base.r = v


class DummyProg:
    def op(self, *a, **k):
        return None

    def dma(self, *a, **k):
        return None


def build_nc(nlayers=NLAYERS, stop=None):
    nc = bass.Bass("TRN2", target_bir_lowering=False)

    def DI(name, shape):
        return nc.dram_tensor(name, list(shape), F32, kind="ExternalInput").ap()

    def DO(name, shape):
        return nc.dram_tensor(name, list(shape), F32, kind="ExternalOutput").ap()

    xT_d = DI("xT", [2048, 1024])
    small_d = DI("small", [128, 512])
    w_ada_d = DI("w_ada", [4, 2048, 6144])
    w_in_e_d = DI("w_in_even", [2, 2048, 6656])
    w_out_e_d = DI("w_out_even", [2, 2048, 2048])
    w_in_o_d = DI("w_in_odd", [2, 2048, 8192])
    w_out_o_d = DI("w_out_odd", [2, 2048, 2048])
    cka_d = DI("ckaT", [2, 128, 2, 256])
    cva_d = DI("cva", [2, 128, 2, 2, 128])
    ckb_d = DI("ckbT", [2, 128, 8, 256])
    cvb_d = DI("cvb", [2, 4, 128, 2, 256])
    srf_d = DI("srf", [2, 8, 128, 2, 256])
    srb_d = DI("srb", [2, 8, 128, 2, 256])
    biasA_d = DI("biasA", [8, 128, 5, 128])
    biasB_d = DI("biasB", [2, 8, 128, 8 * 7 * 128])
    rope_d = DI("rope", [2, 128, 1024])
    c128_d = DI("c128", [8, 128, 128])
    c128b_d = DI("c128b", [2, 128, 128])

    yT_d = DO("yT", [2048, 1024])
    nak_d = DO("nakT", [2, 2, 128, 1024])
    nav_d = DO("nav", [2, 1024, 256])
    nbk_d = DO("nbkT", [2, 8, 128, 1024])
    nbv_d = DO("nbv", [2, 1024, 1024])
    nsf_d = DO("nsf", [2, 8, 4, 2, 128, 256])
    nsb_d = DO("nsb", [2, 8, 4, 2, 128, 256])

    NA = 24576
    with contextlib.ExitStack() as st:
        def sbt(name, shape, dt):
            return st.enter_context(nc.sbuf_tensor(name, list(shape), dt))

        xT = sbt("xT_sb", [128, 16, 1024], F32)
        hT = sbt("hT_sb", [128, 16, 1024], BF16)
        ogT = sbt("ogT_sb", [128, 16, 1024], BF16)
        wbuf = [sbt("w%d" % i, [128, 16, 256], BF16) for i in range(3)]
        arena = sbt("arena", [128, NA], BF16)
        af32 = arena.bitcast(F32)
        small = sbt("small_sb", [128, 512], F32)
        ident_bf = sbt("ident_bf", [128, 128], BF16)
        ones_bf = sbt("ones_bf", [128, 128], BF16)
        perm_f = sbt("perm_f", [128, 128], F32)
        cs_bf = sbt("cs_bf", [128, 16], BF16)
        modTs = [sbt("modT%d" % i, [128, 48], F32) for i in range(2)]
        modS = sbt("modS", [128, 48], F32)
        acol = sbt("acol", [128, 16], F32)
        gpcol = sbt("gpcol", [128, 16], F32)
        exsink = sbt("exsink", [128, 16], F32)
        rowbuf = [sbt("rowbuf%d" % i, [1, 256], F32) for i in range(2)]
        one1 = sbt("one1", [1, 1], F32)
        epscol = sbt("epscol", [128, 1], F32)
        lg = sbt("lg", [128, 16], F32)
        cdec = sbt("cdec", [128, 16], F32)
        kdcol = sbt("kdcol", [128, 16], F32)
        mfb = sbt("mfb", [128, 16], F32)
        st6s = [sbt("st6_%d" % i, [128, 6], F32) for i in range(2)]
        mvs = [sbt("mv_%d" % i, [128, 2], F32) for i in range(2)]
        rs1s = [sbt("rs1_%d" % i, [128, 1], F32) for i in range(2)]
        nb1s = [sbt("nb1_%d" % i, [128, 1], F32) for i in range(2)]

        psum = [st.enter_context(nc.psum_tensor("ps%d" % i, [128, 512], F32)) for i in range(8)]

        realP = Prog(nc)
        T = Tile

        SM = {}
        off = [0]

        def sm(name, n):
            SM[name] = (off[0], n)
            off[0] += n

        sm("cvT", 16)
        sm("b_ada", 4 * 48)
        sm("npre", 4 * 16)
        sm("npost", 4 * 16)
        sm("gn", 2 * 16)
        sm("sink", 2 * 8)
        sm("dec", 2 * 16)
        sm("keepF", 8)
        sm("keepB", 8)
        sm("jtab", 16)
        assert off[0] <= 512

        def smc(name, a=0, n=None):
            o, ln = SM[name]
            if n is None:
                n = ln - a
            return small[:, o + a:o + a + n]

        t_small = T(small)
        t_ident = T(ident_bf)
        t_ones = T(ones_bf)
        t_perm = T(perm_f)
        t_cs = T(cs_bf)
        t_modTs = [T(m) for m in modTs]
        t_modS = T(modS)
        t_acol = T(acol)
        t_gpcol = T(gpcol)
        t_exsink = T(exsink)
        t_row = [T(r) for r in rowbuf]
        t_one1 = T(one1)
        t_epscol = T(epscol)
        t_lg = T(lg)
        t_cdec = T(cdec)
        t_kdcol = T(kdcol)
        t_mfb = T(mfb)
        t_st6 = [T(a) for a in st6s]
        t_mv = [T(a) for a in mvs]
        t_rs1 = [T(a) for a in rs1s]
        t_nb1 = [T(a) for a in nb1s]
        xT_t = [T(xT[:, k, :]) for k in range(16)]
        hT_t = [T(hT[:, k, :]) for k in range(16)]
        ogT_t = [T(ogT[:, k, :]) for k in range(16)]
        w_t = [T(w) for w in wbuf]
        ps_t = [T(p) for p in psum]
        psb_t = [AliasTile(ps_t[i], psum[i].bitcast(BF16)) for i in range(8)]

        state = {}

        def gen(P, dry):
            PSP = {"pj": [0, 1], "sc": [2, 3], "ov": [4], "dn": [5], "mi": [6], "x7": [7], "ada": [7]}
            psrr = {k: 0 for k in PSP}

            def PS(pool):
                i = psrr[pool]
                psrr[pool] = (i + 1) % len(PSP[pool])
                return ps_t[PSP[pool][i]]

            def set_pools(**kw):
                for k, v in kw.items():
                    PSP[k] = v
                    psrr[k] = 0

            class Arena:
                def __init__(self):
                    self.prev = []
                    self.cur = []
                    self.off = 0

                def new_phase(self):
                    self.prev = self.cur
                    self.cur = []
                    self.off = 0

                def alloc(self, shape, dt, name=""):
                    n = 1
                    for s in shape:
                        n *= s
                    nbytes = n * (2 if dt == BF16 else 4)
                    assert self.off % 4 == 0
                    if dt == BF16:
                        e0 = self.off // 2
                        ap = arena[:, e0:e0 + n]
                        handle, rowlen = arena, NA
                    else:
                        e0 = self.off // 4
                        ap = af32[:, e0:e0 + n]
                        handle, rowlen = af32, NA // 2
                    if len(shape) == 2:
                        ap = ap.rearrange("p (a b) -> p a b", a=shape[0])
                    elif len(shape) == 3:
                        ap = ap.rearrange("p (a b c) -> p a b c", a=shape[0], b=shape[1])
                    self.off += (nbytes + 3) // 4 * 4
                    assert self.off <= NA * 2, ("arena overflow", name, self.off)
                    t = Tile(ap, name, off=e0, rowlen=rowlen, handle=handle)
                    t.inherit(self.prev)
                    self.cur.append(t)
                    return t

            AR = Arena()

            def bc_ap(t, elem_off, pattern):
                return bass.AP(t.handle, t.off + elem_off, [[t.rowlen, 128]] + pattern)

            wk = [0]
            wissued = [0]

            def load_w(src_ap):
                k = wk[0]
                wk[0] += 1
                if dry:
                    state.setdefault("wseq", []).append(src_ap)
                    return w_t[k % 3]
                seq = state["wseq"]
                while wissued[0] < min(k + 3, len(seq)):
                    m = wissued[0]
                    P.dma("pool", wbuf[m % 3][:, :, :], seq[m], writes=[w_t[m % 3]])
                    wissued[0] += 1
                return w_t[k % 3]

            def wview(d):
                return d.rearrange("(kc p) n -> p kc n", p=128)

            def proj_fm(wt, off_, src, src_t, evac, split=False):
                for half in range(2):
                    ps = PS("pj")
                    items = [(ps.ap[:, :], wt.ap[:, kc, off_:off_ + 128], src[:, kc, half * 512:(half + 1) * 512], kc == 0, kc == 15)
                             for kc in range(16)]
                    if split and half == 0:
                        for kc in range(16):
                            P.op("pe", mmg([items[kc]]), reads=[wt, src_t[kc]], writes=[ps])
                    else:
                        P.op("pe", mmg(items), reads=[wt] + src_t, writes=[ps])
                    evac(half, ps)

            def proj_tm(wt, ncols, evac):
                for t in range(8):
                    ps = PS("pj")
                    items = [(ps.ap[:, 0:ncols], hT[:, kc, t * 128:(t + 1) * 128], wt.ap[:, kc, 0:ncols], kc == 0, kc == 15)
                             for kc in range(16)]
                    P.op("pe", mmg(items), reads=[wt] + hT_t, writes=[ps])
                    evac(t, ps)

            def ada_gen(l):
                wv = wview(w_ada_d[l])
                mT = modTs[l % 2]
                t_mT = t_modTs[l % 2]
                for t in range(24):
                    wt = load_w(wv[:, :, t * 256:(t + 1) * 256])
                    ps = PS("ada")
                    P.op("pe", mmg([(ps.ap[0:1, 0:256], cs_bf[:, kc:kc + 1], wt.ap[:, kc, :], kc == 0, kc == 15) for kc in range(16)]),
                         reads=[wt, t_cs], writes=[ps])
                    rb = t_row[t % 2]
                    P.op("act", actf(rb.ap[0:1, :], ps.ap[0:1, 0:256], AF.Copy), reads=[ps], writes=[rb])
                    ps2 = PS("ada")
                    P.op("pe", mmg([(ps2.ap[:, n:n + 1], rb.ap[0:1, n * 128:(n + 1) * 128], one1[0:1, 0:1], True, True)
                                    for n in range(2)]), reads=[rb, t_one1], writes=[ps2])
                    P.op("act", actf(mT[:, 2 * t:2 * t + 2], ps2.ap[:, 0:2], AF.Copy), reads=[ps2], writes=[t_mT])
                    yield

            ada_state = {"gens": {}, "done": 0, "cur": 0}

            def ada_step(n=1, force=False):
                for _ in range(n):
                    gi = ada_state["done"] // 24
                    if gi >= nlayers or (gi > ada_state["cur"] + 1 and not force):
                        return
                    if gi not in ada_state["gens"]:
                        ada_state["gens"][gi] = ada_gen(gi)
                    next(ada_state["gens"][gi])
                    ada_state["done"] += 1

            def ada_need(count):
                while ada_state["done"] < count:
                    ada_step(force=True)

            def pipeline(n, stage1, stage2, D, every=0):
                for s in range(n + D):
                    if s < n:
                        stage1(s)
                    if s - D >= 0:
                        stage2(s - D)
                    if every and s % every == every - 1:
                        ada_step()

            P.dma("sp", small[:, :], small_d[:, :], writes=[t_small])
            P.dma("pool", ident_bf[:, :], c128_d[0], writes=[t_ident])
            P.dma("pool", ones_bf[:, :], c128_d[1], writes=[t_ones])
            P.dma("sp", perm_f[:, :], c128_d[2], writes=[t_perm])
            xv = xT_d.rearrange("(kc p) t -> p kc t", p=128)
            for kc in range(16):
                P.dma("sp", xT[:, kc, :], xv[:, kc, :], writes=[xT_t[kc]])
            P.op("dve", lambda e: e.memset(one1[:, :], 1.0), writes=[t_one1])
            P.op("dve", lambda e: e.memset(epscol[:, :], EPS), writes=[t_epscol])
            P.op("act", actf(cs_bf[:, :], smc("cvT"), AF.Silu), reads=[t_small], writes=[t_cs])

            for l in range(nlayers):
                li = l // 2
                even = (l % 2 == 0)
                set_pools(pj=[0, 1], sc=[2, 3], ov=[4], dn=[5], mi=[6], ada=([7] if even else [6]))
                ada_state["cur"] = l
                AR.new_phase()
                sq = [AR.alloc([1024], BF16, "sq%d" % i) for i in range(2)]
                rstd = AR.alloc([1024], F32, "rstd")
                tmpf = [AR.alloc([1024], F32, "tmpf%d" % i) for i in range(2)]
                ssA, ssB = PS("ov"), PS("dn")
                for kc in range(16 if l == 0 else 0):
                    s_ = sq[kc % 2]
                    P.op("act", actf(s_.ap, xT[:, kc, :], AF.Square), reads=[xT_t[kc]], writes=[s_])
                    P.op("pe", mmg([(ssA.ap[:, :], ones_bf[:, :], s_.ap[:, 0:512], kc == 0, kc == 15),
                                    (ssB.ap[:, :], ones_bf[:, :], s_.ap[:, 512:1024], kc == 0, kc == 15)]),
                         reads=[s_, t_ones], writes=[ssA, ssB])
                ada_need(24 * l + 16)
                mT = modTs[l % 2]
                t_mT = t_modTs[l % 2]
                P.op("dve", tt(modS[:, 0:32], mT[:, 0:32], smc("b_ada", l * 48, 32), ALU.add), reads=[t_mT, t_small], writes=[t_modS])
                P.op("dve", stt(acol[:, :], modS[:, 16:32], 1.0, smc("npre", l * 16, 16), ALU.add, ALU.mult),
                     reads=[t_modS, t_small], writes=[t_acol])

                P.op("dve", ts(rstd.ap[:, 0:512], ssA.ap[:, :], 1.0 / 2048, EPS, ALU.mult, ALU.add), reads=[ssA], writes=[rstd])
                P.op("dve", ts(rstd.ap[:, 512:1024], ssB.ap[:, :], 1.0 / 2048, EPS, ALU.mult, ALU.add), reads=[ssB, rstd], writes=[rstd])
                P.op("act", actf(rstd.ap, rstd.ap, AF.Ln), reads=[rstd], writes=[rstd])
                P.op("act", actf(rstd.ap, rstd.ap, AF.Exp, scale=-0.5), reads=[rstd], writes=[rstd])
                for kc in range(16):
                    tm = tmpf[kc % 2]
                    P.op("dve", stt(tm.ap, xT[:, kc, :], acol[:, kc:kc + 1], rstd.ap, ALU.mult, ALU.mult),
                         reads=[xT_t[kc], t_acol, rstd], writes=[tm])
                    P.op("act", actf(hT[:, kc, :], tm.ap, AF.Identity, bias=modS[:, kc:kc + 1]), reads=[tm, t_modS], writes=[hT_t[kc]])

                if even:
                    wv = wview(w_in_e_d[li])
                    AR.new_phase()
                    KTa = AR.alloc([2, 1024], BF16, "KTa")
                    Va = AR.alloc([8, 256], BF16, "Va")
                    cKa = AR.alloc([2, 256], BF16, "cKa")
                    cVa = AR.alloc([2, 256], BF16, "cVa")
                    bA = [AR.alloc([5, 128], BF16, "bA%d" % i) for i in range(2)]
                    QT = AR.alloc([4, 1024], BF16, "QT")
                    cosT = AR.alloc([1024], F32, "cos")
                    sinT = AR.alloc([1024], F32, "sin")
                    qfs = [AR.alloc([512], F32, "qf%d" % i) for i in range(2)]
                    t2s = [AR.alloc([512], F32, "t2%d" % i) for i in range(2)]
                    PT = [AR.alloc([512], BF16, "PT%d" % i) for i in range(3)]
                    rden = AR.alloc([512], F32, "rden")
                    sinkrep = AR.alloc([512], F32, "sinkrep")
                    vfs = [AR.alloc([256], F32, "vf%d" % i) for i in range(2)]
                    sgt = [AR.alloc([512], BF16, "sgt%d" % i) for i in range(2)]
                    P.dma("sp", cosT.ap, rope_d[0], writes=[cosT])
                    P.dma("sp", sinT.ap, rope_d[1], writes=[sinT])
                    P.dma("pool", cKa.ap, cka_d[li], writes=[cKa])
                    P.dma("pool", cVa.ap, cva_d[li].rearrange("p kb n d -> p kb (n d)"), writes=[cVa])
                    P.op("act", actf(exsink[:, :], smc("sink"), AF.Exp), reads=[t_small], writes=[t_exsink])

                    rpend = []
                    rcnt = [0]

                    def rope_flush():
                        while rpend:
                            rpend.pop(0)()

                    def rope_evac(dst_ap_fn, dst_t, out_dram_fn=None, qscale=1.0):
                        def ev(half, ps):
                            qf = qfs[rcnt[0] % 2]
                            t2 = t2s[rcnt[0] % 2]
                            rcnt[0] += 1
                            P.op("act", actf(qf.ap, ps.ap[:, :], AF.Copy), reads=[ps], writes=[qf])
                            if out_dram_fn is not None:
                                P.dma("sp", out_dram_fn(half), qf.ap, reads=[qf], is_output=True)
                            rope_flush()

                            def rest(qf=qf, t2=t2, half=half):
                                p2 = PS("mi")
                                P.op("pe", mmg([(p2.ap[:, :], perm_f[:, :], qf.ap, True, True)]), reads=[qf, t_perm], writes=[p2])
                                cs_ = cosT.ap[:, half * 512:(half + 1) * 512]
                                sn_ = sinT.ap[:, half * 512:(half + 1) * 512]
                                P.op("dve", stt(t2.ap, p2.ap[:, :], qscale, sn_, ALU.mult, ALU.mult), reads=[p2, sinT], writes=[t2])
                                P.op("dve", tt(qf.ap, qf.ap, cs_, ALU.mult), reads=[qf, cosT], writes=[qf])
                                P.op("dve", stt(dst_ap_fn(half), qf.ap, qscale, t2.ap, ALU.mult, ALU.add), reads=[qf, t2], writes=[dst_t])
                            rpend.append(rest)
                        return ev

                    wt = load_w(wv[:, :, 1024:1280])
                    for n in range(2):
                        proj_fm(wt, n * 128, hT, hT_t,
                                rope_evac(lambda half, n=n: KTa.ap[:, n, half * 512:(half + 1) * 512], KTa,
                                          lambda half, n=n: nak_d[li, n, :, half * 512:(half + 1) * 512]), split=(n == 0))
                    pass
                    rope_flush()
                    wt = load_w(wv[:, :, 1280:1536])

                    def va_evac(t, ps):
                        vf = vfs[t % 2]
                        P.op("act", actf(vf.ap, ps.ap[:, 0:256], AF.Copy), reads=[ps], writes=[vf])
                        P.dma("sp", nav_d[li, t * 128:(t + 1) * 128, :], vf.ap, reads=[vf], is_output=True)
                        P.op("dve", cpy(Va.ap[:, t, :], vf.ap), reads=[vf], writes=[Va])
                    proj_tm(wt, 256, va_evac)
                    pass

                    for n in range(2):
                        for tq in range(2):
                            wt = load_w(wv[:, :, n * 512 + tq * 256:n * 512 + (tq + 1) * 256])
                            for bb in range(2):
                                g = tq * 2 + bb
                                proj_fm(wt, bb * 128, hT, hT_t,
                                        rope_evac(lambda half, g=g: QT.ap[:, g, half * 512:(half + 1) * 512], QT, None, SCALE))
                            pass
                        rope_flush()
                        for g in range(4):
                            P.op("act", actf(sinkrep.ap[:, g * 128:(g + 1) * 128], cosT.ap[:, 0:128], AF.Identity,
                                             bias=exsink[:, li * 8 + n * 4 + g:li * 8 + n * 4 + g + 1], scale=0.0),
                                 reads=[cosT, t_exsink], writes=[sinkrep])
                        set_pools(sc=[2, 3, 6], ov=[4, 0], dn=[5, 1])
                        units = [(j, idx) for j in range(8) for idx in range(5)]
                        ovdn = {}

                        def kinfo(j, idx, n=n):
                            kb0 = min(max(j - 1, 0), 5)
                            if idx < 2:
                                kb = idx
                                return (cKa.ap[:, n, kb * 128:(kb + 1) * 128], cKa, cVa.ap[:, kb, n * 128:(n + 1) * 128], cVa)
                            kb = kb0 + idx - 2
                            return (KTa.ap[:, n, kb * 128:(kb + 1) * 128], KTa, Va.ap[:, kb, n * 128:(n + 1) * 128], Va)

                        def a_stage1(s):
                            j, idx = units[s]
                            bj = bA[j % 2]
                            if idx == 0 and j + 1 < 8:
                                P.dma("pool", bA[(j + 1) % 2].ap, biasA_d[j + 1], writes=[bA[(j + 1) % 2]])
                            lhsT, lt, _, _ = kinfo(j, idx)
                            sc = PS("sc")
                            items = [(sc.ap[:, :].rearrange("p (g q) -> p g q", g=4), lhsT, QT.ap[:, :, j * 128:(j + 1) * 128], True, False)]
                            for g_ in range(4):
                                items.append((sc.ap[:, g_ * 128:(g_ + 1) * 128], ident_bf[:, :], bj.ap[:, idx, :], False, g_ == 3))
                            P.op("pe", mmg(items), reads=[lt, QT, bj, t_ident], writes=[sc])
                            pt = PT[s % 3]
                            P.op("act", actf(pt.ap, sc.ap[:, :], AF.Exp), reads=[sc], writes=[pt])

                        def a_stage2(s, n=n):
                            j, idx = units[s]
                            if idx == 0:
                                ovdn[j] = (PS("ov"), PS("dn"))
                            ov, dn = ovdn[j]
                            _, _, vsrc, vt = kinfo(j, idx)
                            pt = PT[s % 3]
                            P.op("pe", mmg([(ov.ap[:, :], vsrc, pt.ap, idx == 0, idx == 4),
                                            (dn.ap[:, :], ones_bf[:, :], pt.ap, idx == 0, idx == 4)]),
                                 reads=[pt, vt, t_ones], writes=[ov, dn])
                            if idx == 4:
                                P.op("dve", tt(rden.ap, dn.ap[:, :], sinkrep.ap, ALU.add), reads=[dn, sinkrep], writes=[rden])
                                P.op("act", actf(rden.ap, rden.ap, AF.Ln), reads=[rden], writes=[rden])
                                P.op("act", actf(rden.ap, rden.ap, AF.Exp, scale=-1.0), reads=[rden], writes=[rden])
                                P.op("dve", tt(ogT[:, 4 * n:4 * n + 4, j * 128:(j + 1) * 128], ov.ap[:, :].rearrange("p (g q) -> p g q", g=4),
                                               rden.ap.rearrange("p (g q) -> p g q", g=4), ALU.mult),
                                     reads=[ov, rden], writes=ogT_t[4 * n:4 * n + 4])

                        P.dma("pool", bA[0].ap, biasA_d[0], writes=[bA[0]])
                        pipeline(len(units), a_stage1, a_stage2, 2, every=7)
                        set_pools(pj=[0, 1], sc=[2, 3], ov=[4], dn=[5], mi=[6])
                        for tq in range(2):
                            c0 = 4608 + n * 512 + tq * 256
                            wt = load_w(wv[:, :, c0:c0 + 256])
                            for bb in range(2):
                                ch = 4 * n + tq * 2 + bb

                                def gate_evac(half, ps, ch=ch):
                                    s_ = sgt[half]
                                    P.op("act", actf(s_.ap, ps.ap[:, :], AF.Silu), reads=[ps], writes=[s_])
                                    P.op("dve", tt(ogT[:, ch, half * 512:(half + 1) * 512], ogT[:, ch, half * 512:(half + 1) * 512], s_.ap, ALU.mult),
                                         reads=[s_, ogT_t[ch]], writes=[ogT_t[ch]])
                                proj_fm(wt, bb * 128, hT, hT_t, gate_evac)
                            pass

                    if stop == "A":
                        break
                    for pp in range(4):
                        AR.new_phase()
                        QTb = AR.alloc([2, 1024], BF16, "QTb")
                        KTb = AR.alloc([2, 1024], BF16, "KTb")
                        Vb = AR.alloc([8, 256], BF16, "Vb")
                        cKb = AR.alloc([2, 256], BF16, "cKb")
                        cVb = AR.alloc([2, 256], BF16, "cVb")
                        bBs = [AR.alloc([896], BF16, "bB%d" % i) for i in range(4)]
                        kf = [AR.alloc([512], F32, "kf%d" % i) for i in range(4)]
                        vfbs = [AR.alloc([256], F32, "vfb%d" % i) for i in range(4)]

                        def issue_bias(q_, pp=pp):
                            hh_, j_ = divmod(q_, 8)
                            b_ = bBs[q_ % 4]
                            P.dma("pool", b_.ap, biasB_d[li, 2 * pp + hh_][:, j_ * 896:(j_ + 1) * 896], writes=[b_])
                        PTb = [AR.alloc([512], BF16, "PTb%d" % i) for i in range(3)]
                        rdb = AR.alloc([512], F32, "rdb")
                        sgb = [AR.alloc([512], BF16, "sgb%d" % i) for i in range(2)]
                        P.dma("pool", cKb.ap, ckb_d[li, :, 2 * pp:2 * pp + 2, :], writes=[cKb])
                        P.dma("pool", cVb.ap, cvb_d[li, pp], writes=[cVb])
                        for q_ in range(3):
                            issue_bias(q_)
                        c0 = 1536 + pp * 256
                        wt = load_w(wv[:, :, c0:c0 + 256])
                        for hh in range(2):
                            def q_evac(half, ps, hh=hh):
                                P.op("act", actf(QTb.ap[:, hh, half * 512:(half + 1) * 512], ps.ap[:, :], AF.Copy, scale=SCALE), reads=[ps], writes=[QTb])
                            proj_fm(wt, hh * 128, hT, hT_t, q_evac)
                        pass
                        c0 = 2560 + pp * 256
                        wt = load_w(wv[:, :, c0:c0 + 256])
                        for hh in range(2):
                            def k_evac(half, ps, hh=hh):
                                k_ = kf[hh * 2 + half]
                                P.op("act", actf(k_.ap, ps.ap[:, :], AF.Copy), reads=[ps], writes=[k_])
                                P.dma("sp", nbk_d[li, 2 * pp + hh, :, half * 512:(half + 1) * 512], k_.ap, reads=[k_], is_output=True)
                                P.op("dve", cpy(KTb.ap[:, hh, half * 512:(half + 1) * 512], k_.ap), reads=[k_], writes=[KTb])
                            proj_fm(wt, hh * 128, hT, hT_t, k_evac)
                        pass
                        c0 = 3584 + pp * 256
                        wt = load_w(wv[:, :, c0:c0 + 256])

                        def vb_evac(t, ps):
                            vfb = vfbs[t % 4]
                            P.op("act", actf(vfb.ap, ps.ap[:, 0:256], AF.Copy), reads=[ps], writes=[vfb])
                            P.dma("sp", nbv_d[li, t * 128:(t + 1) * 128, pp * 256:(pp + 1) * 256], vfb.ap, reads=[vfb], is_output=True)
                            P.op("dve", cpy(Vb.ap[:, t, :], vfb.ap), reads=[vfb], writes=[Vb])
                        proj_tm(wt, 256, vb_evac)
                        pass
                        set_pools(sc=[2, 3, 6], ov=[4, 0], dn=[5, 1])
                        for hh in range(2):
                            h = 2 * pp + hh
                            units = [(j, part) for j in range(8) for part in range(2)]
                            ovdn = {}

                            def klist_of(j):
                                kb0 = min(max(j - 2, 0), 3)
                                return [("c", 0), ("c", 1)] + [("l", kb0 + i) for i in range(5)]

                            def b_stage1(s, hh=hh):
                                j, part = units[s]
                                bB = bBs[(hh * 8 + j) % 4]
                                if part == 0 and hh * 8 + j + 3 < 16:
                                    issue_bias(hh * 8 + j + 3)
                                kl = klist_of(j)
                                rng = range(0, 4) if part == 0 else range(4, 7)
                                nk = len(rng)
                                sc = PS("sc")
                                items = [(sc.ap[:, 0:nk * 128], ident_bf[:, :], bB.ap[:, rng[0] * 128:(rng[-1] + 1) * 128], True, False)]
                                for i_, kbi in enumerate(rng):
                                    kind, kb = kl[kbi]
                                    src = cKb if kind == "c" else KTb
                                    items.append((sc.ap[:, i_ * 128:(i_ + 1) * 128], src.ap[:, hh, kb * 128:(kb + 1) * 128],
                                                  QTb.ap[:, hh, j * 128:(j + 1) * 128], False, i_ == nk - 1))
                                P.op("pe", mmg(items), reads=[cKb, KTb, QTb, bB, t_ident], writes=[sc])
                                pt = PTb[s % 3]
                                P.op("act", actf(pt.ap[:, 0:nk * 128], sc.ap[:, 0:nk * 128], AF.Exp), reads=[sc], writes=[pt])

                            def b_stage2(s, hh=hh, h=h):
                                j, part = units[s]
                                jq, jj = j // 4, j % 4
                                if jj == 0 and part == 0:
                                    ovdn[jq] = (PS("ov"), PS("dn"))
                                ov, dn = ovdn[jq]
                                kl = klist_of(j)
                                rng = range(0, 4) if part == 0 else range(4, 7)
                                pt = PTb[s % 3]
                                items = []
                                for i_, kbi in enumerate(rng):
                                    kind, kb = kl[kbi]
                                    if kind == "c":
                                        vsrc = cVb.ap[:, kb, hh * 128:(hh + 1) * 128]
                                    else:
                                        vsrc = Vb.ap[:, kb, hh * 128:(hh + 1) * 128]
                                    items.append((ov.ap[:, jj * 128:(jj + 1) * 128], vsrc, pt.ap[:, i_ * 128:(i_ + 1) * 128], kbi == 0, kbi == 6))
                                    items.append((dn.ap[:, jj * 128:(jj + 1) * 128], ones_bf[:, :], pt.ap[:, i_ * 128:(i_ + 1) * 128], kbi == 0, kbi == 6))
                                P.op("pe", mmg(items), reads=[pt, cVb, Vb, t_ones], writes=[ov, dn])
                                if jj == 3 and part == 1:
                                    P.op("act", actf(rdb.ap, dn.ap[:, :], AF.Ln), reads=[dn], writes=[rdb])
                                    P.op("act", actf(rdb.ap, rdb.ap, AF.Exp, scale=-1.0), reads=[rdb], writes=[rdb])
                                    P.op("dve", tt(ogT[:, 8 + h, jq * 512:(jq + 1) * 512], ov.ap[:, :], rdb.ap, ALU.mult),
                                         reads=[ov, rdb], writes=[ogT_t[8 + h]])

                            pipeline(len(units), b_stage1, b_stage2, 2, every=6)
                        set_pools(pj=[0, 1], sc=[2, 3], ov=[4], dn=[5], mi=[6])
                        c0 = 5632 + pp * 256
                        wt = load_w(wv[:, :, c0:c0 + 256])
                        for hh in range(2):
                            ch = 8 + 2 * pp + hh

                            def gb_evac(half, ps, ch=ch):
                                s_ = sgb[half]
                                P.op("act", actf(s_.ap, ps.ap[:, :], AF.Silu), reads=[ps], writes=[s_])
                                P.op("dve", tt(ogT[:, ch, half * 512:(half + 1) * 512], ogT[:, ch, half * 512:(half + 1) * 512], s_.ap, ALU.mult),
                                     reads=[s_, ogT_t[ch]], writes=[ogT_t[ch]])
                            proj_fm(wt, hh * 128, hT, hT_t, gb_evac)
                        pass
                    wout_v = wview(w_out_e_d[li])
                else:
                    wv = wview(w_in_o_d[li])
                    P.op("act", actf(lg[:, :], smc("dec", li * 16, 16), AF.Exp), reads=[t_small], writes=[t_lg])
                    P.op("dve", ts(lg[:, :], lg[:, :], -1.0, None, ALU.mult), reads=[t_lg], writes=[t_lg])
                    P.op("act", actf(cdec[:, :], lg[:, :], AF.Exp, scale=128.0), reads=[t_lg], writes=[t_cdec])
                    P.op("dve", tt(kdcol[:, :], lg[:, :], smc("jtab"), ALU.mult), reads=[t_lg, t_small], writes=[t_kdcol])
                    P.op("act", actf(kdcol[:, :], kdcol[:, :], AF.Exp), reads=[t_kdcol], writes=[t_kdcol])
                    P.op("dve", ts(kdcol[:, :], kdcol[:, :], 1.0 / 16, None, ALU.mult), reads=[t_kdcol], writes=[t_kdcol])
                    AR.new_phase()
                    tabs = [AR.alloc([128], F32, "tab%d" % i) for i in range(6)]
                    qTs = [AR.alloc([2, 1024], BF16, "qT%d" % i) for i in range(2)]
                    kTs = [AR.alloc([2, 1024], BF16, "kT%d" % i) for i in range(2)]
                    v_ = AR.alloc([8, 256], BF16, "v")
                    sg_ = AR.alloc([2, 1024], BF16, "sg")
                    Dh = AR.alloc([128], F32, "Dh")
                    t1 = AR.alloc([128], F32, "t1")
                    qdF = AR.alloc([128], F32, "qdF")
                    qdB = AR.alloc([128], F32, "qdB")
                    SinB = AR.alloc([8, 512], BF16, "SinB")
                    SinF = [AR.alloc([512], BF16, "SinF%d" % i) for i in range(2)]
                    Sfull = [AR.alloc([512], F32, "Sfull%d" % i) for i in range(2)]
                    kd = [AR.alloc([256], BF16, "kd%d" % i) for i in range(2)]
                    qdf = [AR.alloc([2, 128], BF16, "qdf%d" % i) for i in range(2)]
                    qdb = [AR.alloc([2, 128], BF16, "qdb%d" % i) for i in range(2)]
                    sd = [AR.alloc([128], BF16, "sd%d" % i) for i in range(2)]
                    on = [AR.alloc([256], BF16, "on%d" % i) for i in range(3)]
                    for i, src in enumerate([c128_d[3], c128_d[4], c128_d[5], c128_d[6], c128b_d[0], c128b_d[1]]):
                        P.dma("sp", tabs[i].ap, src, writes=[tabs[i]])
                    dpos, dneg, mkf, mkb, ip1, irev = tabs
                    tbK = psb_t[7]
                    tbO = psb_t[5]

                    def fm_groups(c0, dst, func):
                        holder = {}
                        outs = []
                        for dc in range(2):
                            for half in range(2):
                                def g(dc=dc, half=half):
                                    if "wt" not in holder:
                                        holder["wt"] = load_w(wv[:, :, c0:c0 + 256])
                                    wt = holder["wt"]
                                    ps = PS("pj")
                                    items = [(ps.ap[:, :], wt.ap[:, kc, dc * 128:(dc + 1) * 128], hT[:, kc, half * 512:(half + 1) * 512], kc == 0, kc == 15)
                                             for kc in range(16)]
                                    P.op("pe", mmg(items), reads=[wt] + hT_t, writes=[ps])
                                    P.op("act", actf(dst.ap[:, dc, half * 512:(half + 1) * 512], ps.ap[:, :], func), reads=[ps], writes=[dst])
                                outs.append(g)
                        return outs

                    def ada_fill():
                        def g():
                            ada_step()
                        return g

                    for g in fm_groups(0 * 2048 + 0 * 256, qTs[0], AF.Copy) + fm_groups(1 * 2048 + 0 * 256, kTs[0], AF.Copy):
                        g()
                    for h in range(8):
                        qT_ = qTs[h % 2]
                        kT_ = kTs[h % 2]
                        set_pools(pj=[0, 1], sc=[2, 3], ov=[4], dn=[5], mi=[6])
                        P.dma("sp", Sfull[0].ap.rearrange("p (dc e) -> p dc e", dc=2), srb_d[li, h], writes=[Sfull[0]])
                        lgf = lg[:, h:h + 1]
                        lgb = lg[:, 8 + h:9 + h]
                        P.op("act", actf(t1.ap, dpos.ap, AF.Exp, scale=lgf), reads=[dpos, t_lg], writes=[t1])
                        P.op("dve", tt(t1.ap, t1.ap, mkf.ap, ALU.mult), reads=[t1, mkf], writes=[t1])
                        P.op("act", actf(Dh.ap, dneg.ap, AF.Exp, scale=lgb), reads=[dneg, t_lg], writes=[Dh])
                        P.op("dve", tt(Dh.ap, Dh.ap, mkb.ap, ALU.mult), reads=[Dh, mkb], writes=[Dh])
                        P.op("dve", tt(Dh.ap, Dh.ap, t1.ap, ALU.add), reads=[Dh, t1], writes=[Dh])
                        P.op("act", actf(qdF.ap, ip1.ap, AF.Exp, scale=lgf), reads=[ip1, t_lg], writes=[qdF])
                        P.op("act", actf(qdB.ap, irev.ap, AF.Exp, scale=lgb), reads=[irev, t_lg], writes=[qdB])
                        P.op("dve", ts(mfb[:, 0:8], smc("keepF"), cdec[:, h:h + 1], None, ALU.mult), reads=[t_small, t_cdec], writes=[t_mfb])
                        P.op("dve", ts(mfb[:, 8:16], smc("keepB"), cdec[:, 8 + h:9 + h], None, ALU.mult), reads=[t_small, t_cdec, t_mfb], writes=[t_mfb])
                        c0 = 4096 + h * 256
                        wt = load_w(wv[:, :, c0:c0 + 256])

                        def v_evac(t, ps):
                            P.op("act", actf(v_.ap[:, t, :], ps.ap[:, 0:256], AF.Copy), reads=[ps], writes=[v_])
                        proj_tm(wt, 256, v_evac)
                        fill = fm_groups(6144 + h * 256, sg_, AF.Silu)
                        fill.append(ada_fill())
                        if h + 1 < 8:
                            fill += fm_groups(0 * 2048 + (h + 1) * 256, qTs[(h + 1) % 2], AF.Copy)
                            fill.append(ada_fill())
                            fill += fm_groups(1 * 2048 + (h + 1) * 256, kTs[(h + 1) % 2], AF.Copy)
                        fill.append(ada_fill())

                        def filler():
                            if fill:
                                fill.pop(0)()

                        set_pools(pj=[0], sc=[2, 3], mi=[6], ov=[4, 1])

                        def st_T(c):
                            P.op("pe", trg([(tbK.ap[:, dc * 128:(dc + 1) * 128], kT_.ap[:, dc, c * 128:(c + 1) * 128], ident_bf[:, :]) for dc in range(2)]),
                                 reads=[kT_, t_ident], writes=[tbK])

                        def st_kd(c, kdc, rot):
                            kd_ = kd[rot % 2]
                            P.op("act", actf(kd_.ap, tbK.ap[:, 0:256], AF.Copy, scale=kdc), reads=[tbK, t_kdcol], writes=[kd_])
                            return kd_

                        def st_U(c, kd_, prev, new, mcol):
                            U = PS("sc")
                            P.op("pe", mmg([(U.ap[:, dc * 256:(dc + 1) * 256], kd_.ap[:, dc * 128:(dc + 1) * 128], v_.ap[:, c, :], True, True) for dc in range(2)]),
                                 reads=[kd_, v_], writes=[U])
                            P.op("dve", stt(new.ap, prev.ap, mcol, U.ap[:, :], ALU.mult, ALU.add), reads=[prev, U, t_mfb], writes=[new])

                        order = list(range(7, -1, -1))
                        kds = {}
                        st_T(order[0])
                        kds[order[0]] = st_kd(order[0], kdcol[:, 8 + h:9 + h], 0)
                        prev = Sfull[0]
                        for ci, c in enumerate(order):
                            new = Sfull[1 - (ci % 2)]
                            P.op("act", actf(SinB.ap[:, c, :], prev.ap, AF.Copy, scale=smc("keepB", c, 1)), reads=[prev, t_small], writes=[SinB])
                            if ci + 1 < 8:
                                cn = order[ci + 1]
                                st_T(cn)
                                kds[cn] = st_kd(cn, kdcol[:, 8 + h:9 + h], ci + 1)
                            st_U(c, kds[c], prev, new, mfb[:, 8 + c:9 + c])
                            if c % 2 == 0:
                                P.dma("sp", nsb_d[li, h, c // 2].rearrange("dc p e -> p dc e"), new.ap.rearrange("p (dc e) -> p dc e", dc=2),
                                      reads=[new], is_output=True)
                            prev = new
                            filler()
                        P.dma("sp", Sfull[1].ap.rearrange("p (dc e) -> p dc e", dc=2), srf_d[li, h], writes=[Sfull[1]])

                        fprev = {}
                        fnew = {}
                        prev = Sfull[1]
                        for c in range(8):
                            fprev[c] = prev
                            fnew[c] = Sfull[0] if prev is Sfull[1] else Sfull[1]
                            prev = fnew[c]
                        kdf = {}

                        def r_A(c):
                            qf_, qb_ = qdf[c % 2], qdb[c % 2]
                            P.op("dve", tt(qf_.ap, qT_.ap[:, :, c * 128:(c + 1) * 128], bc_ap(qdF, 0, [[0, 2], [1, 128]]), ALU.mult),
                                 reads=[qT_, qdF], writes=[qf_])
                            P.op("dve", tt(qb_.ap, qT_.ap[:, :, c * 128:(c + 1) * 128], bc_ap(qdB, 0, [[0, 2], [1, 128]]), ALU.mult),
                                 reads=[qT_, qdB], writes=[qb_])
                            mi = PS("mi")
                            P.op("pe", mmg([(mi.ap[:, 0:128], kT_.ap[:, dc, c * 128:(c + 1) * 128], qT_.ap[:, dc, c * 128:(c + 1) * 128], dc == 0, dc == 1)
                                            for dc in range(2)]), reads=[kT_, qT_], writes=[mi])
                            sd_ = sd[c % 2]
                            P.op("dve", tt(sd_.ap, mi.ap[:, 0:128], Dh.ap, ALU.mult), reads=[mi, Dh], writes=[sd_])
                            st_T(c)
                            kdf[c] = st_kd(c, kdcol[:, h:h + 1], c)

                        def r_D(c):
                            sf = SinF[c % 2]
                            P.op("act", actf(sf.ap, fprev[c].ap, AF.Copy, scale=smc("keepF", c, 1)), reads=[fprev[c], t_small], writes=[sf])
                            qf_, qb_ = qdf[c % 2], qdb[c % 2]
                            sd_ = sd[c % 2]
                            st_U(c, kdf[c], fprev[c], fnew[c], mfb[:, c:c + 1])
                            if c % 2 == 1:
                                P.dma("sp", nsf_d[li, h, c // 2].rearrange("dc p e -> p dc e"), fnew[c].ap.rearrange("p (dc e) -> p dc e", dc=2),
                                      reads=[fnew[c]], is_output=True)
                            ov = PS("ov")
                            items = [(ov.ap[:, 0:256], sd_.ap, v_.ap[:, c, :], True, False)]
                            for dc in range(2):
                                items.append((ov.ap[:, 0:256], qf_.ap[:, dc, :], sf.ap[:, dc * 256:(dc + 1) * 256], False, False))
                            for dc in range(2):
                                items.append((ov.ap[:, 0:256], qb_.ap[:, dc, :], SinB.ap[:, c, dc * 256:(dc + 1) * 256], False, dc == 1))
                            P.op("pe", mmg(items), reads=[sd_, v_, qf_, qb_, sf, SinB], writes=[ov])
                            k2 = c % 2
                            P.op("dve", lambda e, ov=ov, k2=k2: e.bn_stats(out=st6s[k2][:, :], in_=ov.ap[:, 0:256]), reads=[ov], writes=[t_st6[k2]])
                            P.op("dve", lambda e, k2=k2: e.bn_aggr(out=mvs[k2][:, :], in_=st6s[k2][:, :]), reads=[t_st6[k2]], writes=[t_mv[k2]])
                            P.op("act", actf(rs1s[k2][:, :], mvs[k2][:, 1:2], AF.Ln, bias=epscol[:, :]), reads=[t_mv[k2], t_epscol], writes=[t_rs1[k2]])
                            P.op("act", actf(rs1s[k2][:, :], rs1s[k2][:, :], AF.Exp, scale=-0.5), reads=[t_rs1[k2]], writes=[t_rs1[k2]])
                            on_ = on[c % 3]
                            P.op("dve", ts(on_.ap, ov.ap[:, 0:256], mvs[k2][:, 0:1], rs1s[k2][:, :], ALU.subtract, ALU.mult),
                                 reads=[ov, t_mv[k2], t_rs1[k2]], writes=[on_])

                        def r_E(c):
                            on_ = on[c % 3]
                            P.op("pe", trg([(tbO.ap[:, ec * 128:(ec + 1) * 128], on_.ap[:, ec * 128:(ec + 1) * 128], ident_bf[:, :]) for ec in range(2)]),
                                 reads=[on_, t_ident], writes=[tbO])
                            for ec in range(2):
                                ch = 2 * h + ec
                                P.op("dve", stt(ogT[:, ch, c * 128:(c + 1) * 128], tbO.ap[:, ec * 128:(ec + 1) * 128],
                                                smc("gn", li * 16 + ch, 1), sg_.ap[:, ec, c * 128:(c + 1) * 128], ALU.mult, ALU.mult),
                                     reads=[tbO, t_small, sg_], writes=[ogT_t[ch]])

                        for s in range(8 + 3):
                            if s < 8:
                                r_A(s)
                            if 0 <= s - 1 < 8:
                                r_D(s - 1)
                            filler()
                            if 0 <= s - 3 < 8:
                                r_E(s - 3)
                        while fill:
                            filler()
                        set_pools(pj=[0, 1], sc=[2, 3], ov=[4], dn=[5], mi=[6])
                    wout_v = wview(w_out_o_d[li])

                if stop == "B":
                    break
                AR.new_phase()
                sq = [AR.alloc([512], BF16, "osq%d" % i) for i in range(2)]
                rstd = AR.alloc([1024], F32, "orstd")
                tmpf = [AR.alloc([1024], F32, "otmp%d" % i) for i in range(4)]
                xsq = [AR.alloc([1024], BF16, "xsq%d" % i) for i in range(2)]
                ssA, ssB = PS("ov"), PS("dn")
                yT = hT
                yT_t = hT_t
                opend = []

                def o_flush():
                    while opend:
                        ss, s_, n_ = opend.pop(0)
                        P.op("pe", mmg([(ss.ap[:, :], ones_bf[:, :], s_.ap, n_ == 0, n_ == 15)]), reads=[s_, t_ones], writes=[ss])

                ogc = [0]
                for t in range(8):
                    wt = load_w(wout_v[:, :, t * 256:(t + 1) * 256])
                    for bb in range(2):
                        n = 2 * t + bb

                        def o_evac(half, ps, n=n):
                            P.op("act", actf(yT[:, n, half * 512:(half + 1) * 512], ps.ap[:, :], AF.Copy), reads=[ps], writes=[yT_t[n]])
                            s_ = sq[ogc[0] % 2]
                            ogc[0] += 1
                            P.op("act", actf(s_.ap, ps.ap[:, :], AF.Square), reads=[ps], writes=[s_])
                            o_flush()
                            opend.append((ssA if half == 0 else ssB, s_, n))
                        proj_fm(wt, bb * 128, ogT, ogT_t, o_evac)
                    ada_step()
                o_flush()
                P.op("dve", ts(rstd.ap[:, 0:512], ssA.ap[:, :], 1.0 / 2048, EPS, ALU.mult, ALU.add), reads=[ssA], writes=[rstd])
                P.op("dve", ts(rstd.ap[:, 512:1024], ssB.ap[:, :], 1.0 / 2048, EPS, ALU.mult, ALU.add), reads=[ssB, rstd], writes=[rstd])
                P.op("act", actf(rstd.ap, rstd.ap, AF.Ln), reads=[rstd], writes=[rstd])
                P.op("act", actf(rstd.ap, rstd.ap, AF.Exp, scale=-0.5), reads=[rstd], writes=[rstd])
                ada_need(24 * (l + 1))
                P.op("dve", tt(modS[:, 32:48], mT[:, 32:48], smc("b_ada", l * 48 + 32, 16), ALU.add), reads=[t_mT, t_small, t_modS], writes=[t_modS])
                P.op("dve", tt(gpcol[:, :], modS[:, 32:48], smc("npost", l * 16, 16), ALU.mult), reads=[t_modS, t_small], writes=[t_gpcol])
                def scale_op(kc_):
                    tm_ = tmpf[kc_ % 4]
                    P.op("dve", stt(tm_.ap, yT[:, kc_, :], gpcol[:, kc_:kc_ + 1], rstd.ap, ALU.mult, ALU.mult),
                         reads=[yT_t[kc_], t_gpcol, rstd], writes=[tm_])
                scale_op(0)
                scale_op(1)
                for kc in range(16):
                    tm = tmpf[kc % 4]
                    if kc + 2 < 16:
                        scale_op(kc + 2)
                    P.op("dve", tt(xT[:, kc, :], xT[:, kc, :], tm.ap, ALU.add), reads=[tm, xT_t[kc]], writes=[xT_t[kc]])
                    if l + 1 < nlayers:
                        s_ = xsq[kc % 2]
                        P.op("act", actf(s_.ap, xT[:, kc, :], AF.Square), reads=[xT_t[kc]], writes=[s_])
                        P.op("pe", mmg([(ssA.ap[:, :], ones_bf[:, :], s_.ap[:, 0:512], kc == 0, kc == 15),
                                        (ssB.ap[:, :], ones_bf[:, :], s_.ap[:, 512:1024], kc == 0, kc == 15)]),
                             reads=[s_, t_ones], writes=[ssA, ssB])
                ada_step(5)

            yv = yT_d.rearrange("(kc p) t -> p kc t", p=128)
            for kc in range(16):
                P.dma("sp", yv[:, kc, :], xT[:, kc, :], reads=[xT_t[kc]], is_output=True)

        gen(DummyProg(), True)
        gen(realP, False)
        realP.finish()
        realP.emit()
    return nc


def _bias_tables(na_rpb):
    neg = np.float32(NEG)
    bA_s = np.zeros((8, 128, 5, 128), np.float32)
    bA_p = np.zeros((8, 128, 5, 128), np.float32)
    kk = np.arange(128)[:, None]
    qq = np.arange(128)[None, :]
    for j in range(8):
        kb0 = min(max(j - 1, 0), 5)
        bA_p[j, :, 0:2, :] = neg
        for i in range(3):
            kb = kb0 + i
            kpos = kb * 128 + kk
            qpos = j * 128 + qq
            valid = np.abs(qpos - kpos) <= 128
            bA_s[j, :, 2 + i, :] = np.where(valid, np.float32(0), neg)
            bA_p[j, :, 2 + i, :] = np.float32(0) if (kb // 2 == j // 2) else neg
    tq = np.arange(1024)
    r = tq // 64
    c = tq % 64
    r0 = np.clip(r - 4, 0, 8)
    c0 = np.clip(c - 8, 0, 48)
    krow = (tq // 64)[None, :]
    kcol = (tq % 64)[None, :]
    valid = (krow >= r0[:, None]) & (krow < r0[:, None] + 8) & (kcol >= c0[:, None]) & (kcol < c0[:, None] + 16)
    dr = np.clip(krow - r[:, None] + 7, 0, 14)
    dc = np.clip(kcol - c[:, None] + 15, 0, 30)
    bB_s = np.empty((2, 8, 128, 8, 7, 128), np.float32)
    bB_p = np.empty((8, 128, 8, 7, 128), np.float32)
    for i in range(2):
        full = na_rpb[i][:, dr, dc]
        full = np.where(valid[None], full, neg)
        for j in range(8):
            kb0 = min(max(j - 2, 0), 3)
            bB_s[i, :, :, j, 0:2, :] = 0.0
            for t in range(5):
                kb = kb0 + t
                blk = full[:, j * 128:(j + 1) * 128, kb * 128:(kb + 1) * 128]
                bB_s[i, :, :, j, 2 + t, :] = blk.transpose(0, 2, 1)
    for j in range(8):
        kb0 = min(max(j - 2, 0), 3)
        bB_p[:, :, j, 0:2, :] = neg
        for t in range(5):
            kb = kb0 + t
            bB_p[:, :, j, 2 + t, :] = np.float32(0) if (kb // 2 == j // 2) else neg
    bB_p2 = np.ascontiguousarray(np.broadcast_to(bB_p[None], (2,) + bB_p.shape))
    return bA_s, bA_p, bB_s.reshape(2, 8, 128, -1), bB_p2.reshape(2, 8, 128, -1)


def _const_tables():
    ident = np.eye(128, dtype=np.float32)
    ones = np.ones((128, 128), np.float32)
    d = np.arange(128)
    partner = np.where((d % 64) < 32, d + 32, d - 32)
    perm = np.zeros((128, 128), np.float32)
    perm[partner, d] = 1.0
    jj = np.arange(128)[:, None].astype(np.float32)
    ii = np.arange(128)[None, :].astype(np.float32)
    dpos = np.maximum(ii - jj, 0).astype(np.float32)
    dneg = np.maximum(jj - ii, 0).astype(np.float32)
    mkf = ((ii >= jj).astype(np.float32) / 16).astype(np.float32)
    mkb = ((jj >= ii).astype(np.float32) / 16).astype(np.float32)
    ip1 = np.broadcast_to(ii + 1, (128, 128)).astype(np.float32)
    irev = np.broadcast_to(128 - ii, (128, 128)).astype(np.float32)
    c128 = np.stack([ident, ones, perm, dpos, dneg, mkf, mkb, ones])
    c128b = np.stack([ip1, irev])
    t = np.arange(1024)
    inv = (np.float32(10000.0) ** (-np.arange(32, dtype=np.float32) / 32)).astype(np.float32)
    pos = np.where(d[:, None] < 64, (t // 64)[None, :], (t % 64)[None, :]).astype(np.float32)
    ang = (pos * inv[d % 32][:, None]).astype(np.float32)
    cos = np.cos(ang).astype(np.float32)
    sin = np.sin(ang).astype(np.float32)
    sgn = np.where((d % 64) < 32, -1.0, 1.0).astype(np.float32)[:, None]
    rope_s = np.stack([cos, sin * sgn]).astype(np.float32)
    rope_p = np.stack([np.ones_like(cos), np.zeros_like(sin)]).astype(np.float32)
    return c128, c128b, rope_s, rope_p


def _colT(v):
    v = np.asarray(v, np.float32)
    lead = v.shape[:-1]
    n = v.shape[-1] // 128
    a = v.reshape(lead + (n, 128))
    a = np.moveaxis(a, -1, 0)
    return np.ascontiguousarray(a).reshape(128, -1)


def prepare(x_prompt, x_sample, c, cache_a_k, cache_a_v, cache_b_k, cache_b_v, state_ret_f, state_ret_b,
            c_ctx, w_ada, b_ada, norm_pre, norm_post, w_in_even, w_out_even, a_sink, na_rpb,
            w_in_odd, w_out_odd, ret_decay_f, ret_decay_b, ret_gn):
    f = lambda a: np.ascontiguousarray(np.asarray(a, dtype=np.float32))
    x_prompt, x_sample, c, c_ctx = f(x_prompt), f(x_sample), f(c), f(c_ctx)
    w_ada, w_in_even, w_out_even, w_in_odd, w_out_odd = f(w_ada), f(w_in_even), f(w_out_even), f(w_in_odd), f(w_out_odd)
    cache_a_k, cache_a_v, cache_b_k, cache_b_v = f(cache_a_k), f(cache_a_v), f(cache_b_k), f(cache_b_v)
    state_ret_f, state_ret_b = f(state_ret_f), f(state_ret_b)
    na_rpb = f(na_rpb)

    c128, c128b, rope_s, rope_p = _const_tables()
    bA_s, bA_p, bB_s, bB_p = _bias_tables(na_rpb)

    def small_tab(cvec, prompt):
        cols = []
        cols.append(_colT(cvec))
        cols.append(_colT(f(b_ada)))
        cols.append(_colT(f(norm_pre)))
        cols.append(_colT(f(norm_post)))
        cols.append(_colT(f(ret_gn)))
        cols.append(np.broadcast_to(f(a_sink).reshape(1, 16), (128, 16)))
        dec = np.stack([f(ret_decay_f), f(ret_decay_b)], axis=1).reshape(1, 32)
        cols.append(np.broadcast_to(dec, (128, 32)))
        keep = np.ones(7, np.float32)
        if prompt:
            keep[1::2] = 0.0
        keepF = np.concatenate([[1.0], keep]).astype(np.float32)
        keepB = np.concatenate([keep, [1.0]]).astype(np.float32)
        cols.append(np.broadcast_to(keepF[None], (128, 8)))
        cols.append(np.broadcast_to(keepB[None], (128, 8)))
        p = np.arange(128, dtype=np.float32)[:, None]
        jtab = np.concatenate([np.broadcast_to(127 - p, (128, 8)), np.broadcast_to(p, (128, 8))], axis=1)
        cols.append(jtab)
        tab = np.concatenate([np.asarray(a, np.float32) for a in cols], axis=1)
        out = np.zeros((128, 512), np.float32)
        out[:, :tab.shape[1]] = tab
        return out

    zeros_cka = np.zeros((2, 128, 2, 256), np.float32)
    zeros_cva = np.zeros((2, 128, 2, 2, 128), np.float32)
    zeros_ckb = np.zeros((2, 128, 8, 256), np.float32)
    zeros_cvb = np.zeros((2, 4, 128, 2, 256), np.float32)
    zeros_sr = np.zeros((2, 8, 128, 2, 256), np.float32)

    in_maps = []
    for core in range(8):
        prompt = core < 4
        if prompt:
            xt = x_prompt[4 * core:4 * core + 4].reshape(1024, 2048)
            cvec = c_ctx
            m = {"ckaT": zeros_cka, "cva": zeros_cva, "ckbT": zeros_ckb, "cvb": zeros_cvb, "srf": zeros_sr, "srb": zeros_sr,
                 "biasA": bA_p, "biasB": bB_p, "rope": rope_p}
        else:
            b = core - 4
            xt = x_sample[b]
            cvec = c[b]
            cka = np.ascontiguousarray(cache_a_k[b].transpose(0, 3, 1, 2))
            cva = np.ascontiguousarray(cache_a_v[b].reshape(2, 2, 2, 128, 128).transpose(0, 3, 2, 1, 4))
            ckb = np.ascontiguousarray(cache_b_k[b].transpose(0, 3, 1, 2))
            cvb = np.ascontiguousarray(cache_b_v[b].reshape(2, 4, 2, 2, 128, 128).transpose(0, 1, 4, 3, 2, 5)).reshape(2, 4, 128, 2, 256)
            srf = np.ascontiguousarray(state_ret_f[b].reshape(2, 8, 2, 128, 256).transpose(0, 1, 3, 2, 4))
            srb = np.ascontiguousarray(state_ret_b[b].reshape(2, 8, 2, 128, 256).transpose(0, 1, 3, 2, 4))
            m = {"ckaT": cka, "cva": cva, "ckbT": ckb, "cvb": cvb, "srf": srf, "srb": srb,
                 "biasA": bA_s, "biasB": bB_s, "rope": rope_s}
        m.update({"xT": np.ascontiguousarray(xt.T), "small": small_tab(cvec, prompt),
                  "w_ada": w_ada, "w_in_even": w_in_even, "w_out_even": w_out_even,
                  "w_in_odd": w_in_odd, "w_out_odd": w_out_odd, "c128": c128, "c128b": c128b})
        in_maps.append(m)
    return in_maps


def kernel(**inputs):
    in_maps = prepare(**inputs)
    nc = build_nc()
    res = run_bass_kernel_spmd(nc, in_maps, core_ids=list(range(8)))
    R = res.results

    y_prompt = np.concatenate([np.asarray(R[k]["yT"]).T.reshape(4, 256, 2048) for k in range(4)], axis=0)
    y_sample = np.stack([np.asarray(R[k]["yT"]).T for k in range(4, 8)], axis=0)

    def gather(fn):
        return np.ascontiguousarray(np.concatenate([fn(R[k]) for k in range(4)], axis=0)).astype(np.float32)

    new_a_k = gather(lambda r: np.asarray(r["nakT"]).reshape(2, 2, 128, 4, 256).transpose(3, 0, 1, 4, 2))
    new_a_v = gather(lambda r: np.asarray(r["nav"]).reshape(2, 4, 256, 2, 128).transpose(1, 0, 3, 2, 4))
    new_b_k = gather(lambda r: np.asarray(r["nbkT"]).reshape(2, 8, 128, 4, 256).transpose(3, 0, 1, 4, 2))
    new_b_v = gather(lambda r: np.asarray(r["nbv"]).reshape(2, 4, 256, 8, 128).transpose(1, 0, 3, 2, 4))
    new_ret_f = gather(lambda r: np.asarray(r["nsf"]).reshape(2, 8, 4, 256, 256).transpose(2, 0, 1, 3, 4))
    new_ret_b = gather(lambda r: np.asarray(r["nsb"]).reshape(2, 8, 4, 256, 256).transpose(2, 0, 1, 3, 4))
    return (y_prompt.astype(np.float32), y_sample.astype(np.float32), new_a_k, new_a_v, new_b_k, new_b_v, new_ret_f, new_ret_b)
```

```python
import contextlib
import numpy as np
import concourse.bass as bass
import concourse.mybir as mybir
from concourse.bass_utils import run_bass_kernel_spmd

F32 = mybir.dt.float32
BF16 = mybir.dt.bfloat16
AF = mybir.ActivationFunctionType
ALU = mybir.AluOpType

ENGS = ("pe", "act", "dve", "pool", "sp")
NEG = -30000.0
EPS = 1e-6
SCALE = 128.0 ** -0.5
NLAYERS = 4
import os
KDBG = os.environ.get("KDBG", "").split(",")


class Ev:
    __slots__ = ("eng", "idx", "sem", "val")

    def __init__(self, eng=None, idx=None, sem=None, val=None):
        self.eng = eng
        self.idx = idx
        self.sem = sem
        self.val = val


class Tile:
    def __init__(self, ap, name="", off=0, rowlen=0, handle=None):
        self.ap = ap
        self.name = name
        self.w = None
        self.r = []
        self.off = off
        self.rowlen = rowlen
        self.handle = handle

    def inherit(self, others):
        best = {}
        for o in others:
            evs = list(o.r)
            if o.w is not None:
                evs.append(o.w)
            for ev in evs:
                if ev.eng is not None:
                    k = ("c", ev.eng)
                    if k not in best or best[k].idx < ev.idx:
                        best[k] = ev
                else:
                    k = ("d", ev.sem)
                    if k not in best or best[k].val < ev.val:
                        best[k] = ev
        self.r.extend(best.values())
        return self


class OpRec:
    __slots__ = ("fn", "deps", "dma_sem", "dma_val", "signaled")

    def __init__(self, fn, deps):
        self.fn = fn
        self.deps = deps
        self.dma_sem = None
        self.dma_val = None
        self.signaled = False


class Prog:
    def __init__(self, nc, n_dma_sems=12):
        self.nc = nc
        self.ops = {e: [] for e in ENGS}
        self.n_dma_sems = n_dma_sems
        self.dma_rr = {"sp": 0, "pool": 0}
        self.dma_cnt = {}
        self.out_events = []

    def _deps(self, reads, writes):
        deps = []
        for t in reads:
            if t.w is not None:
                deps.append(t.w)
        for t in writes:
            if t.w is not None:
                deps.append(t.w)
            deps.extend(t.r)
        return deps

    def op(self, eng, fn, reads=(), writes=()):
        rec = OpRec(fn, self._deps(reads, writes))
        idx = len(self.ops[eng])
        self.ops[eng].append(rec)
        ev = Ev(eng=eng, idx=idx)
        for t in reads:
            t.r.append(ev)
        for t in writes:
            t.w = ev
            t.r = []
        return ev

    def dma(self, q, out_ap, in_ap, reads=(), writes=(), is_output=False):
        if "poolall" in KDBG or ("poolout" in KDBG and is_output):
            q = "pool"
        slot = self.dma_rr[q]
        self.dma_rr[q] = (slot + 1) % self.n_dma_sems
        key = (q, slot)
        cnt = self.dma_cnt.get(key, 0) + 1
        self.dma_cnt[key] = cnt
        deps = self._deps(reads, writes)
        if cnt > 1:
            deps.append(Ev(sem=key, val=16 * (cnt - 1)))

        def fn(e, out_ap=out_ap, in_ap=in_ap):
            return e.dma_start(out=out_ap, in_=in_ap)

        rec = OpRec(fn, deps)
        rec.dma_sem = key
        rec.dma_val = 16 * cnt
        self.ops[q].append(rec)
        ev = Ev(sem=key, val=16 * cnt)
        for t in reads:
            t.r.append(ev)
        for t in writes:
            t.w = ev
            t.r = []
        if is_output:
            self.out_events.append(ev)
        return ev

    def finish(self):
        rec = OpRec(None, list(self.out_events))
        self.ops["sp"].append(rec)

    def emit(self):
        nc = self.nc
        for e in ENGS:
            for rec in self.ops[e]:
                for d in rec.deps:
                    if d.eng is not None:
                        self.ops[d.eng][d.idx].signaled = True
        counts = {}
        for e in ENGS:
            c = 0
            arr = []
            for rec in self.ops[e]:
                if rec.signaled:
                    c += 1
                arr.append(c)
            counts[e] = arr
        with contextlib.ExitStack() as st:
            psem = {e: st.enter_context(nc.semaphore("p_" + e)) for e in ENGS}
            dsem = {}
            for q in ("sp", "pool"):
                for s in range(self.n_dma_sems):
                    if (q, s) in self.dma_cnt:
                        dsem[(q, s)] = st.enter_context(nc.semaphore("d_%s_%d" % (q, s)))
            block = st.enter_context(nc.Block())

            def run(ename, eh):
                waited = {}
                for rec in self.ops[ename]:
                    for d in rec.deps:
                        if d.eng is not None:
                            if d.eng == ename and ename == "pe":
                                continue
                            k = ("c", d.eng)
                            v = counts[d.eng][d.idx]
                            sem = psem[d.eng]
                        else:
                            k = ("d", d.sem)
                            v = d.val
                            sem = dsem[d.sem]
                        if waited.get(k, 0) >= v:
                            continue
                        waited[k] = v
                        eh.wait_ge(sem, v)
                    if rec.fn is None:
                        continue
                    ins = rec.fn(eh)
                    if rec.dma_sem is not None:
                        ins.then_inc(dsem[rec.dma_sem], 16)
                    elif rec.signaled:
                        ins.then_inc(psem[ename], 1)

            @block.tensor
            def _(eh):
                run("pe", eh)

            @block.scalar
            def _(eh):
                run("act", eh)

            @block.vector
            def _(eh):
                run("dve", eh)

            @block.gpsimd
            def _(eh):
                run("pool", eh)

            @block.sync
            def _(eh):
                run("sp", eh)


def mmg(items):
    def fn(e):
        ins = None
        for (o, l, r, s, t) in items:
            ins = e.matmul(o, l, r, start=s, stop=t)
        return ins
    return fn


def trg(items):
    def fn(e):
        ins = None
        for (o, i, ident) in items:
            ins = e.transpose(o, i, ident)
        return ins
    return fn


def actf(out, in_, func, bias=None, scale=None):
    kw = {}
    if bias is not None:
        kw["bias"] = bias
    if scale is not None:
        kw["scale"] = scale
    return lambda e: e.activation(out=out, in_=in_, func=func, **kw)


def tt(out, in0, in1, op):
    return lambda e: e.tensor_tensor(out=out, in0=in0, in1=in1, op=op)


def ts(out, in0, s1, s2, op0, op1=None):
    if op1 is None:
        return lambda e: e.tensor_scalar(out=out, in0=in0, scalar1=s1, scalar2=None, op0=op0)
    return lambda e: e.tensor_scalar(out=out, in0=in0, scalar1=s1, scalar2=s2, op0=op0, op1=op1)


def stt(out, in0, scalar, in1, op0, op1):
    return lambda e: e.scalar_tensor_tensor(out=out, in0=in0, scalar=scalar, in1=in1, op0=op0, op1=op1)


def cpy(out, in_):
    return lambda e: e.tensor_copy(out, in_)


def recip(out, in_):
    return lambda e: e.reciprocal(out=out, in_=in_)


class AliasTile(Tile):
    def __init__(self, base, ap):
        self.base = base
        self.ap = ap
        self.name = base.name
        self.off = base.off
        self.rowlen = base.rowlen
        self.handle = base.handle

    @property
    def w(self):
        return self.base.w

    @w.setter
    def w(self, v):
        self.base.w = v

    @property
    def r(self):
        return self.base.r

    @r.setter
    def r(self, v):
        self.base.r = v


class DummyProg:
    def op(self, *a, **k):
        return None

    def dma(self, *a, **k):
        return None


def build_nc(nlayers=NLAYERS, stop=None):
    nc = bass.Bass("TRN2", target_bir_lowering=False)

    def DI(name, shape):
        return nc.dram_tensor(name, list(shape), F32, kind="ExternalInput").ap()

    def DO(name, shape):
        return nc.dram_tensor(name, list(shape), F32, kind="ExternalOutput").ap()

    xT_d = DI("xT", [2048, 1024])
    small_d = DI("small", [128, 512])
    w_ada_d = DI("w_ada", [4, 2048, 6144])
    w_in_e_d = DI("w_in_even", [2, 2048, 6656])
    w_out_e_d = DI("w_out_even", [2, 2048, 2048])
    w_in_o_d = DI("w_in_odd", [2, 2048, 8192])
    w_out_o_d = DI("w_out_odd", [2, 2048, 2048])
    cka_d = DI("ckaT", [2, 128, 2, 256])
    cva_d = DI("cva", [2, 128, 2, 2, 128])
    ckb_d = DI("ckbT", [2, 128, 8, 256])
    cvb_d = DI("cvb", [2, 4, 128, 2, 256])
    srf_d = DI("srf", [2, 8, 128, 2, 256])
    srb_d = DI("srb", [2, 8, 128, 2, 256])
    biasA_d = DI("biasA", [8, 128, 5, 128])
    biasB_d = DI("biasB", [2, 8, 128, 8 * 7 * 128])
    rope_d = DI("rope", [2, 128, 1024])
    c128_d = DI("c128", [8, 128, 128])
    c128b_d = DI("c128b", [2, 128, 128])

    yT_d = DO("yT", [2048, 1024])
    nak_d = DO("nakT", [2, 2, 128, 1024])
    nav_d = DO("nav", [2, 1024, 256])
    nbk_d = DO("nbkT", [2, 8, 128, 1024])
    nbv_d = DO("nbv", [2, 1024, 1024])
    nsf_d = DO("nsf", [2, 8, 4, 2, 128, 256])
    nsb_d = DO("nsb", [2, 8, 4, 2, 128, 256])

    NA = 24576
    with contextlib.ExitStack() as st:
        def sbt(name, shape, dt):
            return st.enter_context(nc.sbuf_tensor(name, list(shape), dt))

        xT = sbt("xT_sb", [128, 16, 1024], F32)
        hT = sbt("hT_sb", [128, 16, 1024], BF16)
        ogT = sbt("ogT_sb", [128, 16, 1024], BF16)
        wbuf = [sbt("w%d" % i, [128, 16, 256], BF16) for i in range(3)]
        arena = sbt("arena", [128, NA], BF16)
        af32 = arena.bitcast(F32)
        small = sbt("small_sb", [128, 512], F32)
        ident_bf = sbt("ident_bf", [128, 128], BF16)
        ones_bf = sbt("ones_bf", [128, 128], BF16)
        perm_f = sbt("perm_f", [128, 128], F32)
        cs_bf = sbt("cs_bf", [128, 16], BF16)
        modTs = [sbt("modT%d" % i, [128, 48], F32) for i in range(2)]
        modS = sbt("modS", [128, 48], F32)
        acol = sbt("acol", [128, 16], F32)
        gpcol = sbt("gpcol", [128, 16], F32)
        exsink = sbt("exsink", [128, 16], F32)
        rowbuf = [sbt("rowbuf%d" % i, [1, 256], F32) for i in range(2)]
        one1 = sbt("one1", [1, 1], F32)
        epscol = sbt("epscol", [128, 1], F32)
        lg = sbt("lg", [128, 16], F32)
        cdec = sbt("cdec", [128, 16], F32)
        kdcol = sbt("kdcol", [128, 16], F32)
        mfb = sbt("mfb", [128, 16], F32)
        st6s = [sbt("st6_%d" % i, [128, 6], F32) for i in range(2)]
        mvs = [sbt("mv_%d" % i, [128, 2], F32) for i in range(2)]
        rs1s = [sbt("rs1_%d" % i, [128, 1], F32) for i in range(2)]
        nb1s = [sbt("nb1_%d" % i, [128, 1], F32) for i in range(2)]

        psum = [st.enter_context(nc.psum_tensor("ps%d" % i, [128, 512], F32)) for i in range(8)]

        realP = Prog(nc)
        T = Tile

        SM = {}
        off = [0]

        def sm(name, n):
            SM[name] = (off[0], n)
            off[0] += n

        sm("cvT", 16)
        sm("b_ada", 4 * 48)
        sm("npre", 4 * 16)
        sm("npost", 4 * 16)
        sm("gn", 2 * 16)
        sm("sink", 2 * 8)
        sm("dec", 2 * 16)
        sm("keepF", 8)
        sm("keepB", 8)
        sm("jtab", 16)
        assert off[0] <= 512

        def smc(name, a=0, n=None):
            o, ln = SM[name]
            if n is None:
                n = ln - a
            return small[:, o + a:o + a + n]

        t_small = T(small)
        t_ident = T(ident_bf)
        t_ones = T(ones_bf)
        t_perm = T(perm_f)
        t_cs = T(cs_bf)
        t_modTs = [T(m) for m in modTs]
        t_modS = T(modS)
        t_acol = T(acol)
        t_gpcol = T(gpcol)
        t_exsink = T(exsink)
        t_row = [T(r) for r in rowbuf]
        t_one1 = T(one1)
        t_epscol = T(epscol)
        t_lg = T(lg)
        t_cdec = T(cdec)
        t_kdcol = T(kdcol)
        t_mfb = T(mfb)
        t_st6 = [T(a) for a in st6s]
        t_mv = [T(a) for a in mvs]
        t_rs1 = [T(a) for a in rs1s]
        t_nb1 = [T(a) for a in nb1s]
        xT_t = [T(xT[:, k, :]) for k in range(16)]
        hT_t = [T(hT[:, k, :]) for k in range(16)]
        ogT_t = [T(ogT[:, k, :]) for k in range(16)]
        w_t = [T(w) for w in wbuf]
        ps_t = [T(p) for p in psum]
        psb_t = [AliasTile(ps_t[i], psum[i].bitcast(BF16)) for i in range(8)]

        state = {}

        def gen(P, dry):
            PSP = {"pj": [0, 1], "sc": [2, 3], "ov": [4], "dn": [5], "mi": [6], "x7": [7], "ada": [7]}
            psrr = {k: 0 for k in PSP}

            def PS(pool):
                i = psrr[pool]
                psrr[pool] = (i + 1) % len(PSP[pool])
                return ps_t[PSP[pool][i]]

            def set_pools(**kw):
                for k, v in kw.items():
                    PSP[k] = v
                    psrr[k] = 0

            class Arena:
                def __init__(self):
                    self.prev = []
                    self.cur = []
                    self.off = 0

                def new_phase(self):
                    self.prev = self.cur
                    self.cur = []
                    self.off = 0

                def alloc(self, shape, dt, name=""):
                    n = 1
                    for s in shape:
                        n *= s
                    nbytes = n * (2 if dt == BF16 else 4)
                    assert self.off % 4 == 0
                    if dt == BF16:
                        e0 = self.off // 2
                        ap = arena[:, e0:e0 + n]
                        handle, rowlen = arena, NA
                    else:
                        e0 = self.off // 4
                        ap = af32[:, e0:e0 + n]
                        handle, rowlen = af32, NA // 2
                    if len(shape) == 2:
                        ap = ap.rearrange("p (a b) -> p a b", a=shape[0])
                    elif len(shape) == 3:
                        ap = ap.rearrange("p (a b c) -> p a b c", a=shape[0], b=shape[1])
                    self.off += (nbytes + 3) // 4 * 4
                    assert self.off <= NA * 2, ("arena overflow", name, self.off)
                    t = Tile(ap, name, off=e0, rowlen=rowlen, handle=handle)
                    t.inherit(self.prev)
                    self.cur.append(t)
                    return t

            AR = Arena()

            def bc_ap(t, elem_off, pattern):
                return bass.AP(t.handle, t.off + elem_off, [[t.rowlen, 128]] + pattern)

            wk = [0]
            wissued = [0]

            def load_w(src_ap):
                k = wk[0]
                wk[0] += 1
                if dry:
                    state.setdefault("wseq", []).append(src_ap)
                    return w_t[k % 3]
                seq = state["wseq"]
                while wissued[0] < min(k + 3, len(seq)):
                    m = wissued[0]
                    P.dma("pool", wbuf[m % 3][:, :, :], seq[m], writes=[w_t[m % 3]])
                    wissued[0] += 1
                return w_t[k % 3]

            def wview(d):
                return d.rearrange("(kc p) n -> p kc n", p=128)

            def proj_fm(wt, off_, src, src_t, evac, split=False):
                for half in range(2):
                    ps = PS("pj")
                    items = [(ps.ap[:, :], wt.ap[:, kc, off_:off_ + 128], src[:, kc, half * 512:(half + 1) * 512], kc == 0, kc == 15)
                             for kc in range(16)]
                    if split and half == 0:
                        for kc in range(16):
                            P.op("pe", mmg([items[kc]]), reads=[wt, src_t[kc]], writes=[ps])
                    else:
                        P.op("pe", mmg(items), reads=[wt] + src_t, writes=[ps])
                    evac(half, ps)

            def proj_tm(wt, ncols, evac):
                for t in range(8):
                    ps = PS("pj")
                    items = [(ps.ap[:, 0:ncols], hT[:, kc, t * 128:(t + 1) * 128], wt.ap[:, kc, 0:ncols], kc == 0, kc == 15)
                             for kc in range(16)]
                    P.op("pe", mmg(items), reads=[wt] + hT_t, writes=[ps])
                    evac(t, ps)

            def ada_gen(l):
                wv = wview(w_ada_d[l])
                mT = modTs[l % 2]
                t_mT = t_modTs[l % 2]
                for t in range(24):
                    wt = load_w(wv[:, :, t * 256:(t + 1) * 256])
                    ps = PS("ada")
                    P.op("pe", mmg([(ps.ap[0:1, 0:256], cs_bf[:, kc:kc + 1], wt.ap[:, kc, :], kc == 0, kc == 15) for kc in range(16)]),
                         reads=[wt, t_cs], writes=[ps])
                    rb = t_row[t % 2]
                    P.op("act", actf(rb.ap[0:1, :], ps.ap[0:1, 0:256], AF.Copy), reads=[ps], writes=[rb])
                    ps2 = PS("ada")
                    P.op("pe", mmg([(ps2.ap[:, n:n + 1], rb.ap[0:1, n * 128:(n + 1) * 128], one1[0:1, 0:1], True, True)
                                    for n in range(2)]), reads=[rb, t_one1], writes=[ps2])
                    P.op("act", actf(mT[:, 2 * t:2 * t + 2], ps2.ap[:, 0:2], AF.Copy), reads=[ps2], writes=[t_mT])
                    yield

            ada_state = {"gens": {}, "done": 0, "cur": 0}

            def ada_step(n=1, force=False):
                for _ in range(n):
                    gi = ada_state["done"] // 24
                    if gi >= nlayers or (gi > ada_state["cur"] + 1 and not force):
                        return
                    if gi not in ada_state["gens"]:
                        ada_state["gens"][gi] = ada_gen(gi)
                    next(ada_state["gens"][gi])
                    ada_state["done"] += 1

            def ada_need(count):
                while ada_state["done"] < count:
                    ada_step(force=True)

            def pipeline(n, stage1, stage2, D, every=0):
                for s in range(n + D):
                    if s < n:
                        stage1(s)
                    if s - D >= 0:
                        stage2(s - D)
                    if every and s % every == every - 1:
                        ada_step()

            P.dma("sp", small[:, :], small_d[:, :], writes=[t_small])
            P.dma("pool", ident_bf[:, :], c128_d[0], writes=[t_ident])
            P.dma("pool", ones_bf[:, :], c128_d[1], writes=[t_ones])
            P.dma("sp", perm_f[:, :], c128_d[2], writes=[t_perm])
            xv = xT_d.rearrange("(kc p) t -> p kc t", p=128)
            for kc in range(16):
                P.dma("sp", xT[:, kc, :], xv[:, kc, :], writes=[xT_t[kc]])
            P.op("dve", lambda e: e.memset(one1[:, :], 1.0), writes=[t_one1])
            P.op("dve", lambda e: e.memset(epscol[:, :], EPS), writes=[t_epscol])
            P.op("act", actf(cs_bf[:, :], smc("cvT"), AF.Silu), reads=[t_small], writes=[t_cs])

            for l in range(nlayers):
                li = l // 2
                even = (l % 2 == 0)
                set_pools(pj=[0, 1], sc=[2, 3], ov=[4], dn=[5], mi=[6], ada=([7] if even else [6]))
                ada_state["cur"] = l
                AR.new_phase()
                sq = [AR.alloc([1024], BF16, "sq%d" % i) for i in range(2)]
                rstd = AR.alloc([1024], F32, "rstd")
                tmpf = [AR.alloc([1024], F32, "tmpf%d" % i) for i in range(2)]
                ssA, ssB = PS("ov"), PS("dn")
                for kc in range(16 if l == 0 else 0):
                    s_ = sq[kc % 2]
                    P.op("act", actf(s_.ap, xT[:, kc, :], AF.Square), reads=[xT_t[kc]], writes=[s_])
                    P.op("pe", mmg([(ssA.ap[:, :], ones_bf[:, :], s_.ap[:, 0:512], kc == 0, kc == 15),
                                    (ssB.ap[:, :], ones_bf[:, :], s_.ap[:, 512:1024], kc == 0, kc == 15)]),
                         reads=[s_, t_ones], writes=[ssA, ssB])
                ada_need(24 * l + 16)
                mT = modTs[l % 2]
                t_mT = t_modTs[l % 2]
                P.op("dve", tt(modS[:, 0:32], mT[:, 0:32], smc("b_ada", l * 48, 32), ALU.add), reads=[t_mT, t_small], writes=[t_modS])
                P.op("dve", stt(acol[:, :], modS[:, 16:32], 1.0, smc("npre", l * 16, 16), ALU.add, ALU.mult),
                     reads=[t_modS, t_small], writes=[t_acol])

                P.op("dve", ts(rstd.ap[:, 0:512], ssA.ap[:, :], 1.0 / 2048, EPS, ALU.mult, ALU.add), reads=[ssA], writes=[rstd])
                P.op("dve", ts(rstd.ap[:, 512:1024], ssB.ap[:, :], 1.0 / 2048, EPS, ALU.mult, ALU.add), reads=[ssB, rstd], writes=[rstd])
                P.op("act", actf(rstd.ap, rstd.ap, AF.Ln), reads=[rstd], writes=[rstd])
                P.op("act", actf(rstd.ap, rstd.ap, AF.Exp, scale=-0.5), reads=[rstd], writes=[rstd])
                for kc in range(16):
                    tm = tmpf[kc % 2]
                    P.op("dve", stt(tm.ap, xT[:, kc, :], acol[:, kc:kc + 1], rstd.ap, ALU.mult, ALU.mult),
                         reads=[xT_t[kc], t_acol, rstd], writes=[tm])
                    P.op("act", actf(hT[:, kc, :], tm.ap, AF.Identity, bias=modS[:, kc:kc + 1]), reads=[tm, t_modS], writes=[hT_t[kc]])

                if even:
                    wv = wview(w_in_e_d[li])
                    AR.new_phase()
                    KTa = AR.alloc([2, 1024], BF16, "KTa")
                    Va = AR.alloc([8, 256], BF16, "Va")
                    cKa = AR.alloc([2, 256], BF16, "cKa")
                    cVa = AR.alloc([2, 256], BF16, "cVa")
                    bA = [AR.alloc([5, 128], BF16, "bA%d" % i) for i in range(2)]
                    QT = AR.alloc([4, 1024], BF16, "QT")
                    cosT = AR.alloc([1024], F32, "cos")
                    sinT = AR.alloc([1024], F32, "sin")
                    qfs = [AR.alloc([512], F32, "qf%d" % i) for i in range(2)]
                    t2s = [AR.alloc([512], F32, "t2%d" % i) for i in range(2)]
                    PT = [AR.alloc([512], BF16, "PT%d" % i) for i in range(3)]
                    rden = AR.alloc([512], F32, "rden")
                    sinkrep = AR.alloc([512], F32, "sinkrep")
                    vfs = [AR.alloc([256], F32, "vf%d" % i) for i in range(2)]
                    sgt = [AR.alloc([512], BF16, "sgt%d" % i) for i in range(2)]
                    P.dma("sp", cosT.ap, rope_d[0], writes=[cosT])
                    P.dma("sp", sinT.ap, rope_d[1], writes=[sinT])
                    P.dma("pool", cKa.ap, cka_d[li], writes=[cKa])
                    P.dma("pool", cVa.ap, cva_d[li].rearrange("p kb n d -> p kb (n d)"), writes=[cVa])
                    P.op("act", actf(exsink[:, :], smc("sink"), AF.Exp), reads=[t_small], writes=[t_exsink])

                    rpend = []
                    rcnt = [0]

                    def rope_flush():
                        while rpend:
                            rpend.pop(0)()

                    def rope_evac(dst_ap_fn, dst_t, out_dram_fn=None, qscale=1.0):
                        def ev(half, ps):
                            qf = qfs[rcnt[0] % 2]
                            t2 = t2s[rcnt[0] % 2]
                            rcnt[0] += 1
                            P.op("act", actf(qf.ap, ps.ap[:, :], AF.Copy), reads=[ps], writes=[qf])
                            if out_dram_fn is not None:
                                P.dma("sp", out_dram_fn(half), qf.ap, reads=[qf], is_output=True)
                            rope_flush()

                            def rest(qf=qf, t2=t2, half=half):
                                p2 = PS("mi")
                                P.op("pe", mmg([(p2.ap[:, :], perm_f[:, :], qf.ap, True, True)]), reads=[qf, t_perm], writes=[p2])
                                cs_ = cosT.ap[:, half * 512:(half + 1) * 512]
                                sn_ = sinT.ap[:, half * 512:(half + 1) * 512]
                                P.op("dve", stt(t2.ap, p2.ap[:, :], qscale, sn_, ALU.mult, ALU.mult), reads=[p2, sinT], writes=[t2])
                                P.op("dve", tt(qf.ap, qf.ap, cs_, ALU.mult), reads=[qf, cosT], writes=[qf])
                                P.op("dve", stt(dst_ap_fn(half), qf.ap, qscale, t2.ap, ALU.mult, ALU.add), reads=[qf, t2], writes=[dst_t])
                            rpend.append(rest)
                        return ev

                    wt = load_w(wv[:, :, 1024:1280])
                    for n in range(2):
                        proj_fm(wt, n * 128, hT, hT_t,
                                rope_evac(lambda half, n=n: KTa.ap[:, n, half * 512:(half + 1) * 512], KTa,
                                          lambda half, n=n: nak_d[li, n, :, half * 512:(half + 1) * 512]), split=(n == 0))
                    pass
                    rope_flush()
                    wt = load_w(wv[:, :, 1280:1536])

                    def va_evac(t, ps):
                        vf = vfs[t % 2]
                        P.op("act", actf(vf.ap, ps.ap[:, 0:256], AF.Copy), reads=[ps], writes=[vf])
                        P.dma("sp", nav_d[li, t * 128:(t + 1) * 128, :], vf.ap, reads=[vf], is_output=True)
                        P.op("dve", cpy(Va.ap[:, t, :], vf.ap), reads=[vf], writes=[Va])
                    proj_tm(wt, 256, va_evac)
                    pass

                    for n in range(2):
                        for tq in range(2):
                            wt = load_w(wv[:, :, n * 512 + tq * 256:n * 512 + (tq + 1) * 256])
                            for bb in range(2):
                                g = tq * 2 + bb
                                proj_fm(wt, bb * 128, hT, hT_t,
                                        rope_evac(lambda half, g=g: QT.ap[:, g, half * 512:(half + 1) * 512], QT, None, SCALE))
                            pass
                        rope_flush()
                        for g in range(4):
                            P.op("act", actf(sinkrep.ap[:, g * 128:(g + 1) * 128], cosT.ap[:, 0:128], AF.Identity,
                                             bias=exsink[:, li * 8 + n * 4 + g:li * 8 + n * 4 + g + 1], scale=0.0),
                                 reads=[cosT, t_exsink], writes=[sinkrep])
                        set_pools(sc=[2, 3, 6], ov=[4, 0], dn=[5, 1])
                        units = [(j, idx) for j in range(8) for idx in range(5)]
                        ovdn = {}

                        def kinfo(j, idx, n=n):
                            kb0 = min(max(j - 1, 0), 5)
                            if idx < 2:
                                kb = idx
                                return (cKa.ap[:, n, kb * 128:(kb + 1) * 128], cKa, cVa.ap[:, kb, n * 128:(n + 1) * 128], cVa)
                            kb = kb0 + idx - 2
                            return (KTa.ap[:, n, kb * 128:(kb + 1) * 128], KTa, Va.ap[:, kb, n * 128:(n + 1) * 128], Va)

                        def a_stage1(s):
                            j, idx = units[s]
                            bj = bA[j % 2]
                            if idx == 0 and j + 1 < 8:
                                P.dma("pool", bA[(j + 1) % 2].ap, biasA_d[j + 1], writes=[bA[(j + 1) % 2]])
                            lhsT, lt, _, _ = kinfo(j, idx)
                            sc = PS("sc")
                            items = [(sc.ap[:, :].rearrange("p (g q) -> p g q", g=4), lhsT, QT.ap[:, :, j * 128:(j + 1) * 128], True, False)]
                            for g_ in range(4):
                                items.append((sc.ap[:, g_ * 128:(g_ + 1) * 128], ident_bf[:, :], bj.ap[:, idx, :], False, g_ == 3))
                            P.op("pe", mmg(items), reads=[lt, QT, bj, t_ident], writes=[sc])
                            pt = PT[s % 3]
                            P.op("act", actf(pt.ap, sc.ap[:, :], AF.Exp), reads=[sc], writes=[pt])

                        def a_stage2(s, n=n):
                            j, idx = units[s]
                            if idx == 0:
                                ovdn[j] = (PS("ov"), PS("dn"))
                            ov, dn = ovdn[j]
                            _, _, vsrc, vt = kinfo(j, idx)
                            pt = PT[s % 3]
                            P.op("pe", mmg([(ov.ap[:, :], vsrc, pt.ap, idx == 0, idx == 4),
                                            (dn.ap[:, :], ones_bf[:, :], pt.ap, idx == 0, idx == 4)]),
                                 reads=[pt, vt, t_ones], writes=[ov, dn])
                            if idx == 4:
                                P.op("dve", tt(rden.ap, dn.ap[:, :], sinkrep.ap, ALU.add), reads=[dn, sinkrep], writes=[rden])
                                P.op("act", actf(rden.ap, rden.ap, AF.Ln), reads=[rden], writes=[rden])
                                P.op("act", actf(rden.ap, rden.ap, AF.Exp, scale=-1.0), reads=[rden], writes=[rden])
                                P.op("dve", tt(ogT[:, 4 * n:4 * n + 4, j * 128:(j + 1) * 128], ov.ap[:, :].rearrange("p (g q) -> p g q", g=4),
                                               rden.ap.rearrange("p (g q) -> p g q", g=4), ALU.mult),
                                     reads=[ov, rden], writes=ogT_t[4 * n:4 * n + 4])

                        P.dma("pool", bA[0].ap, biasA_d[0], writes=[bA[0]])
                        pipeline(len(units), a_stage1, a_stage2, 2, every=7)
                        set_pools(pj=[0, 1], sc=[2, 3], ov=[4], dn=[5], mi=[6])
                        for tq in range(2):
                            c0 = 4608 + n * 512 + tq * 256
                            wt = load_w(wv[:, :, c0:c0 + 256])
                            for bb in range(2):
                                ch = 4 * n + tq * 2 + bb

                                def gate_evac(half, ps, ch=ch):
                                    s_ = sgt[half]
                                    P.op("act", actf(s_.ap, ps.ap[:, :], AF.Silu), reads=[ps], writes=[s_])
                                    P.op("dve", tt(ogT[:, ch, half * 512:(half + 1) * 512], ogT[:, ch, half * 512:(half + 1) * 512], s_.ap, ALU.mult),
                                         reads=[s_, ogT_t[ch]], writes=[ogT_t[ch]])
                                proj_fm(wt, bb * 128, hT, hT_t, gate_evac)
                            pass

                    if stop == "A":
                        break
                    for pp in range(4):
                        AR.new_phase()
                        QTb = AR.alloc([2, 1024], BF16, "QTb")
                        KTb = AR.alloc([2, 1024], BF16, "KTb")
                        Vb = AR.alloc([8, 256], BF16, "Vb")
                        cKb = AR.alloc([2, 256], BF16, "cKb")
                        cVb = AR.alloc([2, 256], BF16, "cVb")
                        bBs = [AR.alloc([896], BF16, "bB%d" % i) for i in range(4)]
                        kf = [AR.alloc([512], F32, "kf%d" % i) for i in range(4)]
                        vfbs = [AR.alloc([256], F32, "vfb%d" % i) for i in range(4)]

                        def issue_bias(q_, pp=pp):
                            hh_, j_ = divmod(q_, 8)
                            b_ = bBs[q_ % 4]
                            P.dma("pool", b_.ap, biasB_d[li, 2 * pp + hh_][:, j_ * 896:(j_ + 1) * 896], writes=[b_])
                        PTb = [AR.alloc([512], BF16, "PTb%d" % i) for i in range(3)]
                        rdb = AR.alloc([512], F32, "rdb")
                        sgb = [AR.alloc([512], BF16, "sgb%d" % i) for i in range(2)]
                        P.dma("pool", cKb.ap, ckb_d[li, :, 2 * pp:2 * pp + 2, :], writes=[cKb])
                        P.dma("pool", cVb.ap, cvb_d[li, pp], writes=[cVb])
                        for q_ in range(3):
                            issue_bias(q_)
                        c0 = 1536 + pp * 256
                        wt = load_w(wv[:, :, c0:c0 + 256])
                        for hh in range(2):
                            def q_evac(half, ps, hh=hh):
                                P.op("act", actf(QTb.ap[:, hh, half * 512:(half + 1) * 512], ps.ap[:, :], AF.Copy, scale=SCALE), reads=[ps], writes=[QTb])
                            proj_fm(wt, hh * 128, hT, hT_t, q_evac)
                        pass
                        c0 = 2560 + pp * 256
                        wt = load_w(wv[:, :, c0:c0 + 256])
                        for hh in range(2):
                            def k_evac(half, ps, hh=hh):
                                k_ = kf[hh * 2 + half]
                                P.op("act", actf(k_.ap, ps.ap[:, :], AF.Copy), reads=[ps], writes=[k_])
                                P.dma("sp", nbk_d[li, 2 * pp + hh, :, half * 512:(half + 1) * 512], k_.ap, reads=[k_], is_output=True)
                                P.op("dve", cpy(KTb.ap[:, hh, half * 512:(half + 1) * 512], k_.ap), reads=[k_], writes=[KTb])
                            proj_fm(wt, hh * 128, hT, hT_t, k_evac)
                        pass
                        c0 = 3584 + pp * 256
                        wt = load_w(wv[:, :, c0:c0 + 256])

                        def vb_evac(t, ps):
                            vfb = vfbs[t % 4]
                            P.op("act", actf(vfb.ap, ps.ap[:, 0:256], AF.Copy), reads=[ps], writes=[vfb])
                            P.dma("sp", nbv_d[li, t * 128:(t + 1) * 128, pp * 256:(pp + 1) * 256], vfb.ap, reads=[vfb], is_output=True)
                            P.op("dve", cpy(Vb.ap[:, t, :], vfb.ap), reads=[vfb], writes=[Vb])
                        proj_tm(wt, 256, vb_evac)
                        pass
                        set_pools(sc=[2, 3, 6], ov=[4, 0], dn=[5, 1])
                        for hh in range(2):
                            h = 2 * pp + hh
                            units = [(j, part) for j in range(8) for part in range(2)]
                            ovdn = {}

                            def klist_of(j):
                                kb0 = min(max(j - 2, 0), 3)
                                return [("c", 0), ("c", 1)] + [("l", kb0 + i) for i in range(5)]

                            def b_stage1(s, hh=hh):
                                j, part = units[s]
                                bB = bBs[(hh * 8 + j) % 4]
                                if part == 0 and hh * 8 + j + 3 < 16:
                                    issue_bias(hh * 8 + j + 3)
                                kl = klist_of(j)
                                rng = range(0, 4) if part == 0 else range(4, 7)
                                nk = len(rng)
                                sc = PS("sc")
                                items = [(sc.ap[:, 0:nk * 128], ident_bf[:, :], bB.ap[:, rng[0] * 128:(rng[-1] + 1) * 128], True, False)]
                                for i_, kbi in enumerate(rng):
                                    kind, kb = kl[kbi]
                                    src = cKb if kind == "c" else KTb
                                    items.append((sc.ap[:, i_ * 128:(i_ + 1) * 128], src.ap[:, hh, kb * 128:(kb + 1) * 128],
                                                  QTb.ap[:, hh, j * 128:(j + 1) * 128], False, i_ == nk - 1))
                                P.op("pe", mmg(items), reads=[cKb, KTb, QTb, bB, t_ident], writes=[sc])
                                pt = PTb[s % 3]
                                P.op("act", actf(pt.ap[:, 0:nk * 128], sc.ap[:, 0:nk * 128], AF.Exp), reads=[sc], writes=[pt])

                            def b_stage2(s, hh=hh, h=h):
                                j, part = units[s]
                                jq, jj = j // 4, j % 4
                                if jj == 0 and part == 0:
                                    ovdn[jq] = (PS("ov"), PS("dn"))
                                ov, dn = ovdn[jq]
                                kl = klist_of(j)
                                rng = range(0, 4) if part == 0 else range(4, 7)
                                pt = PTb[s % 3]
                                items = []
                                for i_, kbi in enumerate(rng):
                                    kind, kb = kl[kbi]
                                    if kind == "c":
                                        vsrc = cVb.ap[:, kb, hh * 128:(hh + 1) * 128]
                                    else:
                                        vsrc = Vb.ap[:, kb, hh * 128:(hh + 1) * 128]
                                    items.append((ov.ap[:, jj * 128:(jj + 1) * 128], vsrc, pt.ap[:, i_ * 128:(i_ + 1) * 128], kbi == 0, kbi == 6))
                                    items.append((dn.ap[:, jj * 128:(jj + 1) * 128], ones_bf[:, :], pt.ap[:, i_ * 128:(i_ + 1) * 128], kbi == 0, kbi == 6))
                                P.op("pe", mmg(items), reads=[pt, cVb, Vb, t_ones], writes=[ov, dn])
                                if jj == 3 and part == 1:
                                    P.op("act", actf(rdb.ap, dn.ap[:, :], AF.Ln), reads=[dn], writes=[rdb])
                                    P.op("act", actf(rdb.ap, rdb.ap, AF.Exp, scale=-1.0), reads=[rdb], writes=[rdb])
                                    P.op("dve", tt(ogT[:, 8 + h, jq * 512:(jq + 1) * 512], ov.ap[:, :], rdb.ap, ALU.mult),
                                         reads=[ov, rdb], writes=[ogT_t[8 + h]])

                            pipeline(len(units), b_stage1, b_stage2, 2, every=6)
                        set_pools(pj=[0, 1], sc=[2, 3], ov=[4], dn=[5], mi=[6])
                        c0 = 5632 + pp * 256
                        wt = load_w(wv[:, :, c0:c0 + 256])
                        for hh in range(2):
                            ch = 8 + 2 * pp + hh

                            def gb_evac(half, ps, ch=ch):
                                s_ = sgb[half]
                                P.op("act", actf(s_.ap, ps.ap[:, :], AF.Silu), reads=[ps], writes=[s_])
                                P.op("dve", tt(ogT[:, ch, half * 512:(half + 1) * 512], ogT[:, ch, half * 512:(half + 1) * 512], s_.ap, ALU.mult),
                                     reads=[s_, ogT_t[ch]], writes=[ogT_t[ch]])
                            proj_fm(wt, hh * 128, hT, hT_t, gb_evac)
                        pass
                    wout_v = wview(w_out_e_d[li])
                else:
                    wv = wview(w_in_o_d[li])
                    P.op("act", actf(lg[:, :], smc("dec", li * 16, 16), AF.Exp), reads=[t_small], writes=[t_lg])
                    P.op("dve", ts(lg[:, :], lg[:, :], -1.0, None, ALU.mult), reads=[t_lg], writes=[t_lg])
                    P.op("act", actf(cdec[:, :], lg[:, :], AF.Exp, scale=128.0), reads=[t_lg], writes=[t_cdec])
                    P.op("dve", tt(kdcol[:, :], lg[:, :], smc("jtab"), ALU.mult), reads=[t_lg, t_small], writes=[t_kdcol])
                    P.op("act", actf(kdcol[:, :], kdcol[:, :], AF.Exp), reads=[t_kdcol], writes=[t_kdcol])
                    P.op("dve", ts(kdcol[:, :], kdcol[:, :], 1.0 / 16, None, ALU.mult), reads=[t_kdcol], writes=[t_kdcol])
                    AR.new_phase()
                    tabs = [AR.alloc([128], F32, "tab%d" % i) for i in range(6)]
                    qTs = [AR.alloc([2, 1024], BF16, "qT%d" % i) for i in range(2)]
                    kTs = [AR.alloc([2, 1024], BF16, "kT%d" % i) for i in range(2)]
                    v_ = AR.alloc([8, 256], BF16, "v")
                    sg_ = AR.alloc([2, 1024], BF16, "sg")
                    Dh = AR.alloc([128], F32, "Dh")
                    t1 = AR.alloc([128], F32, "t1")
                    qdF = AR.alloc([128], F32, "qdF")
                    qdB = AR.alloc([128], F32, "qdB")
                    SinB = AR.alloc([8, 512], BF16, "SinB")
                    SinF = [AR.alloc([512], BF16, "SinF%d" % i) for i in range(2)]
                    Sfull = [AR.alloc([512], F32, "Sfull%d" % i) for i in range(2)]
                    kd = [AR.alloc([256], BF16, "kd%d" % i) for i in range(2)]
                    qdf = [AR.alloc([2, 128], BF16, "qdf%d" % i) for i in range(2)]
                    qdb = [AR.alloc([2, 128], BF16, "qdb%d" % i) for i in range(2)]
                    sd = [AR.alloc([128], BF16, "sd%d" % i) for i in range(2)]
                    on = [AR.alloc([256], BF16, "on%d" % i) for i in range(3)]
                    for i, src in enumerate([c128_d[3], c128_d[4], c128_d[5], c128_d[6], c128b_d[0], c128b_d[1]]):
                        P.dma("sp", tabs[i].ap, src, writes=[tabs[i]])
                    dpos, dneg, mkf, mkb, ip1, irev = tabs
                    tbK = psb_t[7]
                    tbO = psb_t[5]

                    def fm_groups(c0, dst, func, split_first=False):
                        holder = {}
                        outs = []
                        for dc in range(2):
                            for half in range(2):
                                def g(dc=dc, half=half):
                                    if "wt" not in holder:
                                        holder["wt"] = load_w(wv[:, :, c0:c0 + 256])
                                    wt = holder["wt"]
                                    ps = PS("pj")
                                    items = [(ps.ap[:, :], wt.ap[:, kc, dc * 128:(dc + 1) * 128], hT[:, kc, half * 512:(half + 1) * 512], kc == 0, kc == 15)
                                             for kc in range(16)]
                                    if split_first and dc == 0 and half == 0:
                                        for kc in range(16):
                                            P.op("pe", mmg([items[kc]]), reads=[wt, hT_t[kc]], writes=[ps])
                                    else:
                                        P.op("pe", mmg(items), reads=[wt] + hT_t, writes=[ps])
                                    P.op("act", actf(dst.ap[:, dc, half * 512:(half + 1) * 512], ps.ap[:, :], func), reads=[ps], writes=[dst])
                                outs.append(g)
                        return outs

                    def ada_fill():
                        def g():
                            ada_step()
                        return g

                    for g in fm_groups(0 * 2048 + 0 * 256, qTs[0], AF.Copy, split_first=True) + fm_groups(1 * 2048 + 0 * 256, kTs[0], AF.Copy):
                        g()
                    for h in range(8):
                        qT_ = qTs[h % 2]
                        kT_ = kTs[h % 2]
                        set_pools(pj=[0, 1], sc=[2, 3], ov=[4], dn=[5], mi=[6])
                        P.dma("sp", Sfull[0].ap.rearrange("p (dc e) -> p dc e", dc=2), srb_d[li, h], writes=[Sfull[0]])
                        lgf = lg[:, h:h + 1]
                        lgb = lg[:, 8 + h:9 + h]
                        P.op("act", actf(t1.ap, dpos.ap, AF.Exp, scale=lgf), reads=[dpos, t_lg], writes=[t1])
                        P.op("dve", tt(t1.ap, t1.ap, mkf.ap, ALU.mult), reads=[t1, mkf], writes=[t1])
                        P.op("act", actf(Dh.ap, dneg.ap, AF.Exp, scale=lgb), reads=[dneg, t_lg], writes=[Dh])
                        P.op("dve", tt(Dh.ap, Dh.ap, mkb.ap, ALU.mult), reads=[Dh, mkb], writes=[Dh])
                        P.op("dve", tt(Dh.ap, Dh.ap, t1.ap, ALU.add), reads=[Dh, t1], writes=[Dh])
                        P.op("act", actf(qdF.ap, ip1.ap, AF.Exp, scale=lgf), reads=[ip1, t_lg], writes=[qdF])
                        P.op("act", actf(qdB.ap, irev.ap, AF.Exp, scale=lgb), reads=[irev, t_lg], writes=[qdB])
                        P.op("dve", ts(mfb[:, 0:8], smc("keepF"), cdec[:, h:h + 1], None, ALU.mult), reads=[t_small, t_cdec], writes=[t_mfb])
                        P.op("dve", ts(mfb[:, 8:16], smc("keepB"), cdec[:, 8 + h:9 + h], None, ALU.mult), reads=[t_small, t_cdec, t_mfb], writes=[t_mfb])
                        c0 = 4096 + h * 256
                        wt = load_w(wv[:, :, c0:c0 + 256])

                        def v_evac(t, ps):
                            P.op("act", actf(v_.ap[:, t, :], ps.ap[:, 0:256], AF.Copy), reads=[ps], writes=[v_])
                        proj_tm(wt, 256, v_evac)
                        fill = fm_groups(6144 + h * 256, sg_, AF.Silu)
                        fill.append(ada_fill())
                        if h + 1 < 8:
                            fill += fm_groups(0 * 2048 + (h + 1) * 256, qTs[(h + 1) % 2], AF.Copy)
                            fill.append(ada_fill())
                            fill += fm_groups(1 * 2048 + (h + 1) * 256, kTs[(h + 1) % 2], AF.Copy)
                        fill.append(ada_fill())

                        def filler():
                            if fill:
                                fill.pop(0)()

                        set_pools(pj=[0], sc=[2, 3], mi=[6], ov=[4, 1])

                        def st_T(c):
                            P.op("pe", trg([(tbK.ap[:, dc * 128:(dc + 1) * 128], kT_.ap[:, dc, c * 128:(c + 1) * 128], ident_bf[:, :]) for dc in range(2)]),
                                 reads=[kT_, t_ident], writes=[tbK])

                        def st_kd(c, kdc, rot):
                            kd_ = kd[rot % 2]
                            P.op("act", actf(kd_.ap, tbK.ap[:, 0:256], AF.Copy, scale=kdc), reads=[tbK, t_kdcol], writes=[kd_])
                            return kd_

                        def st_U(c, kd_, prev, new, mcol):
                            U = PS("sc")
                            P.op("pe", mmg([(U.ap[:, dc * 256:(dc + 1) * 256], kd_.ap[:, dc * 128:(dc + 1) * 128], v_.ap[:, c, :], True, True) for dc in range(2)]),
                                 reads=[kd_, v_], writes=[U])
                            P.op("dve", stt(new.ap, prev.ap, mcol, U.ap[:, :], ALU.mult, ALU.add), reads=[prev, U, t_mfb], writes=[new])

                        order = list(range(7, -1, -1))
                        kds = {}
                        st_T(order[0])
                        kds[order[0]] = st_kd(order[0], kdcol[:, 8 + h:9 + h], 0)
                        prev = Sfull[0]
                        for ci, c in enumerate(order):
                            new = Sfull[1 - (ci % 2)]
                            P.op("act", actf(SinB.ap[:, c, :], prev.ap, AF.Copy, scale=smc("keepB", c, 1)), reads=[prev, t_small], writes=[SinB])
                            if ci + 1 < 8:
                                cn = order[ci + 1]
                                st_T(cn)
                                kds[cn] = st_kd(cn, kdcol[:, 8 + h:9 + h], ci + 1)
                            st_U(c, kds[c], prev, new, mfb[:, 8 + c:9 + c])
                            if c % 2 == 0:
                                P.dma("sp", nsb_d[li, h, c // 2].rearrange("dc p e -> p dc e"), new.ap.rearrange("p (dc e) -> p dc e", dc=2),
                                      reads=[new], is_output=True)
                            prev = new
                            filler()
                        P.dma("sp", Sfull[1].ap.rearrange("p (dc e) -> p dc e", dc=2), srf_d[li, h], writes=[Sfull[1]])

                        fprev = {}
                        fnew = {}
                        prev = Sfull[1]
                        for c in range(8):
                            fprev[c] = prev
                            fnew[c] = Sfull[0] if prev is Sfull[1] else Sfull[1]
                            prev = fnew[c]
                        kdf = {}

                        def r_A(c):
                            qf_, qb_ = qdf[c % 2], qdb[c % 2]
                            P.op("pool", tt(qf_.ap, qT_.ap[:, :, c * 128:(c + 1) * 128], bc_ap(qdF, 0, [[0, 2], [1, 128]]), ALU.mult),
                                 reads=[qT_, qdF], writes=[qf_])
                            P.op("pool", tt(qb_.ap, qT_.ap[:, :, c * 128:(c + 1) * 128], bc_ap(qdB, 0, [[0, 2], [1, 128]]), ALU.mult),
                                 reads=[qT_, qdB], writes=[qb_])
                            mi = PS("mi")
                            P.op("pe", mmg([(mi.ap[:, 0:128], kT_.ap[:, dc, c * 128:(c + 1) * 128], qT_.ap[:, dc, c * 128:(c + 1) * 128], dc == 0, dc == 1)
                                            for dc in range(2)]), reads=[kT_, qT_], writes=[mi])
                            sd_ = sd[c % 2]
                            P.op("dve", tt(sd_.ap, mi.ap[:, 0:128], Dh.ap, ALU.mult), reads=[mi, Dh], writes=[sd_])
                            st_T(c)
                            kdf[c] = st_kd(c, kdcol[:, h:h + 1], c)

                        def r_D(c):
                            sf = SinF[c % 2]
                            P.op("act", actf(sf.ap, fprev[c].ap, AF.Copy, scale=smc("keepF", c, 1)), reads=[fprev[c], t_small], writes=[sf])
                            qf_, qb_ = qdf[c % 2], qdb[c % 2]
                            sd_ = sd[c % 2]
                            st_U(c, kdf[c], fprev[c], fnew[c], mfb[:, c:c + 1])
                            if c % 2 == 1:
                                P.dma("sp", nsf_d[li, h, c // 2].rearrange("dc p e -> p dc e"), fnew[c].ap.rearrange("p (dc e) -> p dc e", dc=2),
                                      reads=[fnew[c]], is_output=True)
                            ov = PS("ov")
                            items = [(ov.ap[:, 0:256], sd_.ap, v_.ap[:, c, :], True, False)]
                            for dc in range(2):
                                items.append((ov.ap[:, 0:256], qf_.ap[:, dc, :], sf.ap[:, dc * 256:(dc + 1) * 256], False, False))
                            for dc in range(2):
                                items.append((ov.ap[:, 0:256], qb_.ap[:, dc, :], SinB.ap[:, c, dc * 256:(dc + 1) * 256], False, dc == 1))
                            P.op("pe", mmg(items), reads=[sd_, v_, qf_, qb_, sf, SinB], writes=[ov])
                            k2 = c % 2
                            P.op("dve", lambda e, ov=ov, k2=k2: e.bn_stats(out=st6s[k2][:, :], in_=ov.ap[:, 0:256]), reads=[ov], writes=[t_st6[k2]])
                            P.op("dve", lambda e, k2=k2: e.bn_aggr(out=mvs[k2][:, :], in_=st6s[k2][:, :]), reads=[t_st6[k2]], writes=[t_mv[k2]])
                            P.op("act", actf(rs1s[k2][:, :], mvs[k2][:, 1:2], AF.Ln, bias=epscol[:, :]), reads=[t_mv[k2], t_epscol], writes=[t_rs1[k2]])
                            P.op("act", actf(rs1s[k2][:, :], rs1s[k2][:, :], AF.Exp, scale=-0.5), reads=[t_rs1[k2]], writes=[t_rs1[k2]])
                            on_ = on[c % 3]
                            P.op("dve", ts(on_.ap, ov.ap[:, 0:256], mvs[k2][:, 0:1], rs1s[k2][:, :], ALU.subtract, ALU.mult),
                                 reads=[ov, t_mv[k2], t_rs1[k2]], writes=[on_])

                        def r_E(c):
                            on_ = on[c % 3]
                            P.op("pe", trg([(tbO.ap[:, ec * 128:(ec + 1) * 128], on_.ap[:, ec * 128:(ec + 1) * 128], ident_bf[:, :]) for ec in range(2)]),
                                 reads=[on_, t_ident], writes=[tbO])
                            for ec in range(2):
                                ch = 2 * h + ec
                                P.op("dve", stt(ogT[:, ch, c * 128:(c + 1) * 128], tbO.ap[:, ec * 128:(ec + 1) * 128],
                                                smc("gn", li * 16 + ch, 1), sg_.ap[:, ec, c * 128:(c + 1) * 128], ALU.mult, ALU.mult),
                                     reads=[tbO, t_small, sg_], writes=[ogT_t[ch]])

                        for s in range(8 + 3):
                            if s < 8:
                                r_A(s)
                            if 0 <= s - 1 < 8:
                                r_D(s - 1)
                            filler()
                            if 0 <= s - 3 < 8:
                                r_E(s - 3)
                        while fill:
                            filler()
                        set_pools(pj=[0, 1], sc=[2, 3], ov=[4], dn=[5], mi=[6])
                    wout_v = wview(w_out_o_d[li])

                if stop == "B":
                    break
                AR.new_phase()
                sq = [AR.alloc([512], BF16, "osq%d" % i) for i in range(2)]
                rstd = AR.alloc([1024], F32, "orstd")
                tmpf = [AR.alloc([1024], F32, "otmp%d" % i) for i in range(4)]
                xsq = [AR.alloc([1024], BF16, "xsq%d" % i) for i in range(2)]
                ssA, ssB = PS("ov"), PS("dn")
                yT = hT
                yT_t = hT_t
                opend = []

                def o_flush():
                    while opend:
                        ss, s_, n_ = opend.pop(0)
                        P.op("pe", mmg([(ss.ap[:, :], ones_bf[:, :], s_.ap, n_ == 0, n_ == 15)]), reads=[s_, t_ones], writes=[ss])

                ogc = [0]
                for t in range(8):
                    wt = load_w(wout_v[:, :, t * 256:(t + 1) * 256])
                    for bb in range(2):
                        n = 2 * t + bb

                        def o_evac(half, ps, n=n):
                            P.op("act", actf(yT[:, n, half * 512:(half + 1) * 512], ps.ap[:, :], AF.Copy), reads=[ps], writes=[yT_t[n]])
                            s_ = sq[ogc[0] % 2]
                            ogc[0] += 1
                            P.op("act", actf(s_.ap, ps.ap[:, :], AF.Square), reads=[ps], writes=[s_])
                            o_flush()
                            opend.append((ssA if half == 0 else ssB, s_, n))
                        proj_fm(wt, bb * 128, ogT, ogT_t, o_evac)
                    ada_step()
                o_flush()
                P.op("dve", ts(rstd.ap[:, 0:512], ssA.ap[:, :], 1.0 / 2048, EPS, ALU.mult, ALU.add), reads=[ssA], writes=[rstd])
                P.op("dve", ts(rstd.ap[:, 512:1024], ssB.ap[:, :], 1.0 / 2048, EPS, ALU.mult, ALU.add), reads=[ssB, rstd], writes=[rstd])
                P.op("act", actf(rstd.ap, rstd.ap, AF.Ln), reads=[rstd], writes=[rstd])
                P.op("act", actf(rstd.ap, rstd.ap, AF.Exp, scale=-0.5), reads=[rstd], writes=[rstd])
                ada_need(24 * (l + 1))
                P.op("dve", tt(modS[:, 32:48], mT[:, 32:48], smc("b_ada", l * 48 + 32, 16), ALU.add), reads=[t_mT, t_small, t_modS], writes=[t_modS])
                P.op("dve", tt(gpcol[:, :], modS[:, 32:48], smc("npost", l * 16, 16), ALU.mult), reads=[t_modS, t_small], writes=[t_gpcol])
                def scale_op(kc_):
                    tm_ = tmpf[kc_ % 4]
                    P.op("dve", stt(tm_.ap, yT[:, kc_, :], gpcol[:, kc_:kc_ + 1], rstd.ap, ALU.mult, ALU.mult),
                         reads=[yT_t[kc_], t_gpcol, rstd], writes=[tm_])
                scale_op(0)
                scale_op(1)
                for kc in range(16):
                    tm = tmpf[kc % 4]
                    if kc + 2 < 16:
                        scale_op(kc + 2)
                    P.op("dve", tt(xT[:, kc, :], xT[:, kc, :], tm.ap, ALU.add), reads=[tm, xT_t[kc]], writes=[xT_t[kc]])
                    if l + 1 < nlayers:
                        s_ = xsq[kc % 2]
                        P.op("act", actf(s_.ap, xT[:, kc, :], AF.Square), reads=[xT_t[kc]], writes=[s_])
                        P.op("pe", mmg([(ssA.ap[:, :], ones_bf[:, :], s_.ap[:, 0:512], kc == 0, kc == 15),
                                        (ssB.ap[:, :], ones_bf[:, :], s_.ap[:, 512:1024], kc == 0, kc == 15)]),
                             reads=[s_, t_ones], writes=[ssA, ssB])
                ada_step(5)

            yv = yT_d.rearrange("(kc p) t -> p kc t", p=128)
            for kc in range(16):
                P.dma("sp", yv[:, kc, :], xT[:, kc, :], reads=[xT_t[kc]], is_output=True)

        gen(DummyProg(), True)
        gen(realP, False)
        realP.finish()
        realP.emit()
    return nc


def _bias_tables(na_rpb):
    neg = np.float32(NEG)
    bA_s = np.zeros((8, 128, 5, 128), np.float32)
    bA_p = np.zeros((8, 128, 5, 128), np.float32)
    kk = np.arange(128)[:, None]
    qq = np.arange(128)[None, :]
    for j in range(8):
        kb0 = min(max(j - 1, 0), 5)
        bA_p[j, :, 0:2, :] = neg
        for i in range(3):
            kb = kb0 + i
            kpos = kb * 128 + kk
            qpos = j * 128 + qq
            valid = np.abs(qpos - kpos) <= 128
            bA_s[j, :, 2 + i, :] = np.where(valid, np.float32(0), neg)
            bA_p[j, :, 2 + i, :] = np.float32(0) if (kb // 2 == j // 2) else neg
    tq = np.arange(1024)
    r = tq // 64
    c = tq % 64
    r0 = np.clip(r - 4, 0, 8)
    c0 = np.clip(c - 8, 0, 48)
    krow = (tq // 64)[None, :]
    kcol = (tq % 64)[None, :]
    valid = (krow >= r0[:, None]) & (krow < r0[:, None] + 8) & (kcol >= c0[:, None]) & (kcol < c0[:, None] + 16)
    dr = np.clip(krow - r[:, None] + 7, 0, 14)
    dc = np.clip(kcol - c[:, None] + 15, 0, 30)
    bB_s = np.empty((2, 8, 128, 8, 7, 128), np.float32)
    bB_p = np.empty((8, 128, 8, 7, 128), np.float32)
    for i in range(2):
        full = na_rpb[i][:, dr, dc]
        full = np.where(valid[None], full, neg)
        for j in range(8):
            kb0 = min(max(j - 2, 0), 3)
            bB_s[i, :, :, j, 0:2, :] = 0.0
            for t in range(5):
                kb = kb0 + t
                blk = full[:, j * 128:(j + 1) * 128, kb * 128:(kb + 1) * 128]
                bB_s[i, :, :, j, 2 + t, :] = blk.transpose(0, 2, 1)
    for j in range(8):
        kb0 = min(max(j - 2, 0), 3)
        bB_p[:, :, j, 0:2, :] = neg
        for t in range(5):
            kb = kb0 + t
            bB_p[:, :, j, 2 + t, :] = np.float32(0) if (kb // 2 == j // 2) else neg
    bB_p2 = np.ascontiguousarray(np.broadcast_to(bB_p[None], (2,) + bB_p.shape))
    return bA_s, bA_p, bB_s.reshape(2, 8, 128, -1), bB_p2.reshape(2, 8, 128, -1)


def _const_tables():
    ident = np.eye(128, dtype=np.float32)
    ones = np.ones((128, 128), np.float32)
    d = np.arange(128)
    partner = np.where((d % 64) < 32, d + 32, d - 32)
    perm = np.zeros((128, 128), np.float32)
    perm[partner, d] = 1.0
    jj = np.arange(128)[:, None].astype(np.float32)
    ii = np.arange(128)[None, :].astype(np.float32)
    dpos = np.maximum(ii - jj, 0).astype(np.float32)
    dneg = np.maximum(jj - ii, 0).astype(np.float32)
    mkf = ((ii >= jj).astype(np.float32) / 16).astype(np.float32)
    mkb = ((jj >= ii).astype(np.float32) / 16).astype(np.float32)
    ip1 = np.broadcast_to(ii + 1, (128, 128)).astype(np.float32)
    irev = np.broadcast_to(128 - ii, (128, 128)).astype(np.float32)
    c128 = np.stack([ident, ones, perm, dpos, dneg, mkf, mkb, ones])
    c128b = np.stack([ip1, irev])
    t = np.arange(1024)
    inv = (np.float32(10000.0) ** (-np.arange(32, dtype=np.float32) / 32)).astype(np.float32)
    pos = np.where(d[:, None] < 64, (t // 64)[None, :], (t % 64)[None, :]).astype(np.float32)
    ang = (pos * inv[d % 32][:, None]).astype(np.float32)
    cos = np.cos(ang).astype(np.float32)
    sin = np.sin(ang).astype(np.float32)
    sgn = np.where((d % 64) < 32, -1.0, 1.0).astype(np.float32)[:, None]
    rope_s = np.stack([cos, sin * sgn]).astype(np.float32)
    rope_p = np.stack([np.ones_like(cos), np.zeros_like(sin)]).astype(np.float32)
    return c128, c128b, rope_s, rope_p


def _colT(v):
    v = np.asarray(v, np.float32)
    lead = v.shape[:-1]
    n = v.shape[-1] // 128
    a = v.reshape(lead + (n, 128))
    a = np.moveaxis(a, -1, 0)
    return np.ascontiguousarray(a).reshape(128, -1)


def prepare(x_prompt, x_sample, c, cache_a_k, cache_a_v, cache_b_k, cache_b_v, state_ret_f, state_ret_b,
            c_ctx, w_ada, b_ada, norm_pre, norm_post, w_in_even, w_out_even, a_sink, na_rpb,
            w_in_odd, w_out_odd, ret_decay_f, ret_decay_b, ret_gn):
    f = lambda a: np.ascontiguousarray(np.asarray(a, dtype=np.float32))
    x_prompt, x_sample, c, c_ctx = f(x_prompt), f(x_sample), f(c), f(c_ctx)
    w_ada, w_in_even, w_out_even, w_in_odd, w_out_odd = f(w_ada), f(w_in_even), f(w_out_even), f(w_in_odd), f(w_out_odd)
    cache_a_k, cache_a_v, cache_b_k, cache_b_v = f(cache_a_k), f(cache_a_v), f(cache_b_k), f(cache_b_v)
    state_ret_f, state_ret_b = f(state_ret_f), f(state_ret_b)
    na_rpb = f(na_rpb)

    c128, c128b, rope_s, rope_p = _const_tables()
    bA_s, bA_p, bB_s, bB_p = _bias_tables(na_rpb)

    def small_tab(cvec, prompt):
        cols = []
        cols.append(_colT(cvec))
        cols.append(_colT(f(b_ada)))
        cols.append(_colT(f(norm_pre)))
        cols.append(_colT(f(norm_post)))
        cols.append(_colT(f(ret_gn)))
        cols.append(np.broadcast_to(f(a_sink).reshape(1, 16), (128, 16)))
        dec = np.stack([f(ret_decay_f), f(ret_decay_b)], axis=1).reshape(1, 32)
        cols.append(np.broadcast_to(dec, (128, 32)))
        keep = np.ones(7, np.float32)
        if prompt:
            keep[1::2] = 0.0
        keepF = np.concatenate([[1.0], keep]).astype(np.float32)
        keepB = np.concatenate([keep, [1.0]]).astype(np.float32)
        cols.append(np.broadcast_to(keepF[None], (128, 8)))
        cols.append(np.broadcast_to(keepB[None], (128, 8)))
        p = np.arange(128, dtype=np.float32)[:, None]
        jtab = np.concatenate([np.broadcast_to(127 - p, (128, 8)), np.broadcast_to(p, (128, 8))], axis=1)
        cols.append(jtab)
        tab = np.concatenate([np.asarray(a, np.float32) for a in cols], axis=1)
        out = np.zeros((128, 512), np.float32)
        out[:, :tab.shape[1]] = tab
        return out

    zeros_cka = np.zeros((2, 128, 2, 256), np.float32)
    zeros_cva = np.zeros((2, 128, 2, 2, 128), np.float32)
    zeros_ckb = np.zeros((2, 128, 8, 256), np.float32)
    zeros_cvb = np.zeros((2, 4, 128, 2, 256), np.float32)
    zeros_sr = np.zeros((2, 8, 128, 2, 256), np.float32)

    in_maps = []
    for core in range(8):
        prompt = core < 4
        if prompt:
            xt = x_prompt[4 * core:4 * core + 4].reshape(1024, 2048)
            cvec = c_ctx
            m = {"ckaT": zeros_cka, "cva": zeros_cva, "ckbT": zeros_ckb, "cvb": zeros_cvb, "srf": zeros_sr, "srb": zeros_sr,
                 "biasA": bA_p, "biasB": bB_p, "rope": rope_p}
        else:
            b = core - 4
            xt = x_sample[b]
            cvec = c[b]
            cka = np.ascontiguousarray(cache_a_k[b].transpose(0, 3, 1, 2))
            cva = np.ascontiguousarray(cache_a_v[b].reshape(2, 2, 2, 128, 128).transpose(0, 3, 2, 1, 4))
            ckb = np.ascontiguousarray(cache_b_k[b].transpose(0, 3, 1, 2))
            cvb = np.ascontiguousarray(cache_b_v[b].reshape(2, 4, 2, 2, 128, 128).transpose(0, 1, 4, 3, 2, 5)).reshape(2, 4, 128, 2, 256)
            srf = np.ascontiguousarray(state_ret_f[b].reshape(2, 8, 2, 128, 256).transpose(0, 1, 3, 2, 4))
            srb = np.ascontiguousarray(state_ret_b[b].reshape(2, 8, 2, 128, 256).transpose(0, 1, 3, 2, 4))
            m = {"ckaT": cka, "cva": cva, "ckbT": ckb, "cvb": cvb, "srf": srf, "srb": srb,
                 "biasA": bA_s, "biasB": bB_s, "rope": rope_s}
        m.update({"xT": np.ascontiguousarray(xt.T), "small": small_tab(cvec, prompt),
                  "w_ada": w_ada, "w_in_even": w_in_even, "w_out_even": w_out_even,
                  "w_in_odd": w_in_odd, "w_out_odd": w_out_odd, "c128": c128, "c128b": c128b})
        in_maps.append(m)
    return in_maps


def kernel(**inputs):
    in_maps = prepare(**inputs)
    nc = build_nc()
    res = run_bass_kernel_spmd(nc, in_maps, core_ids=list(range(8)))
    R = res.results

    y_prompt = np.concatenate([np.asarray(R[k]["yT"]).T.reshape(4, 256, 2048) for k in range(4)], axis=0)
    y_sample = np.stack([np.asarray(R[k]["yT"]).T for k in range(4, 8)], axis=0)

    def gather(fn):
        return np.ascontiguousarray(np.concatenate([fn(R[k]) for k in range(4)], axis=0)).astype(np.float32)

    new_a_k = gather(lambda r: np.asarray(r["nakT"]).reshape(2, 2, 128, 4, 256).transpose(3, 0, 1, 4, 2))
    new_a_v = gather(lambda r: np.asarray(r["nav"]).reshape(2, 4, 256, 2, 128).transpose(1, 0, 3, 2, 4))
    new_b_k = gather(lambda r: np.asarray(r["nbkT"]).reshape(2, 8, 128, 4, 256).transpose(3, 0, 1, 4, 2))
    new_b_v = gather(lambda r: np.asarray(r["nbv"]).reshape(2, 4, 256, 8, 128).transpose(1, 0, 3, 2, 4))
    new_ret_f = gather(lambda r: np.asarray(r["nsf"]).reshape(2, 8, 4, 256, 256).transpose(2, 0, 1, 3, 4))
    new_ret_b = gather(lambda r: np.asarray(r["nsb"]).reshape(2, 8, 4, 256, 256).transpose(2, 0, 1, 3, 4))
    return (y_prompt.astype(np.float32), y_sample.astype(np.float32), new_a_k, new_a_v, new_b_k, new_b_v, new_ret_f, new_ret_b)
```
